# Optimizing a Trainium2 kernel written in Bass

```python
import math
import jax, jax.numpy as jnp
from jax import lax
import numpy as np

D_MODEL = 1024
BATCH = 32
SEQ = 256
DEPTH = 4
DEC_BATCH = 4
DEC_SEQ = 4096
PAST_LEN = 256

GRID_W = 64
SSD_HEADS = 8
SSD_HEAD_DIM = 64
SSD_WIDTH = 512
SSD_GROUPS = 2
SSD_STATE = 128
SSD_BC = SSD_GROUPS * SSD_STATE
SSD_CONV = 5
SSD_CONV_DIM = SSD_WIDTH + 2 * SSD_BC
SSD_CHUNK = 128
GLA_HEADS = 4
GLA_DK = 32
GLA_DV = 64
GLA_KEY_DIM = GLA_HEADS * GLA_DK
GLA_WIDTH = GLA_HEADS * GLA_DV
GLA_LOWRANK = 16
GLA_GATE_NORM = 16.0
GLA_CHUNK = 64
ATT_HEADS = 4
ATT_KV_HEADS = 2
ATT_GROUP = ATT_HEADS // ATT_KV_HEADS
HEAD_DIM = 64
ATT_WIDTH = ATT_HEADS * HEAD_DIM
ATT_KV_DIM = ATT_KV_HEADS * HEAD_DIM
WINDOW = 128
ATT_BLOCK = 128
ROPE_THETA = 10000.0
D_MIX = SSD_WIDTH + GLA_WIDTH + ATT_WIDTH
IN_SIZES = (SSD_WIDTH, SSD_CONV_DIM, 2 * SSD_HEADS,
            GLA_KEY_DIM, GLA_KEY_DIM, GLA_WIDTH, GLA_WIDTH, 2 * GLA_LOWRANK,
            ATT_WIDTH, ATT_KV_DIM, ATT_KV_DIM, ATT_WIDTH)
D_IN = (SSD_WIDTH + SSD_CONV_DIM + 2 * SSD_HEADS
        + 2 * GLA_KEY_DIM + 2 * GLA_WIDTH + 2 * GLA_LOWRANK
        + 2 * ATT_WIDTH + 2 * ATT_KV_DIM)
EPS = 1e-6
NEG_INF = -1e30

kernel_name = "hybrid_ssd_gla_swa_prefix_dit_step"


def rms_norm(x, w):
    xf = x.astype(jnp.float32)
    y = xf * lax.rsqrt(jnp.mean(xf * xf, axis=-1, keepdims=True) + EPS)
    return (y * w.astype(jnp.float32)).astype(x.dtype)


def flip(t):
    return jnp.flip(t, axis=1)


def split_proj(p):
    parts, start = [], 0
    for size in IN_SIZES:
        parts.append(p[..., start:start + size])
        start += size
    return parts


def modulation(cond, w_ada, b_ada):
    mod = jax.nn.silu(cond) @ w_ada + b_ada
    return mod[..., :D_MODEL], mod[..., D_MODEL:2 * D_MODEL], mod[..., 2 * D_MODEL:]


def depthwise_conv(u, w, b):
    ch = u.shape[-1]
    pad = (SSD_CONV - 1) // 2
    out = lax.conv_general_dilated(u, w[:, None, :].astype(u.dtype), window_strides=(1,),
                                   padding=[(pad, pad)], dimension_numbers=('NWC', 'WIO', 'NWC'),
                                   feature_group_count=ch)
    return out + b


def ssd_scan(x, a, bm, cm, h0):
    b, l, h, p = x.shape
    n = bm.shape[-1]
    nc = l // SSD_CHUNK
    x = x.reshape(b, nc, SSD_CHUNK, h, p)
    a = a.reshape(b, nc, SSD_CHUNK, h)
    bm = bm.reshape(b, nc, SSD_CHUNK, h, n)
    cm = cm.reshape(b, nc, SSD_CHUNK, h, n)
    a_cs = jnp.cumsum(a, axis=2)
    tri = jnp.tril(jnp.ones((SSD_CHUNK, SSD_CHUNK), dtype=bool))[None, None, :, :, None]
    seg = a_cs[:, :, :, None, :] - a_cs[:, :, None, :, :]
    decay = jnp.where(tri, jnp.exp(jnp.where(tri, seg, 0.0)), 0.0)
    scores = jnp.einsum('bcqhn,bcshn->bcqsh', cm, bm) * decay
    y_diag = jnp.einsum('bcqsh,bcshp->bcqhp', scores, x)
    a_last = a_cs[:, :, -1]
    states = jnp.einsum('bcshn,bcsh,bcshp->bchpn', bm, jnp.exp(a_last[:, :, None] - a_cs), x)

    def step(carry, inp):
        st, dec = inp
        return carry * dec[:, :, None, None] + st, carry

    h_fin, h_enter = lax.scan(step, h0, (jnp.moveaxis(states, 1, 0), jnp.moveaxis(jnp.exp(a_last), 1, 0)))
    h_enter = jnp.moveaxis(h_enter, 0, 1)
    y_off = jnp.einsum('bcqhn,bchpn,bcqh->bcqhp', cm, h_enter, jnp.exp(a_cs))
    return (y_diag + y_off).reshape(b, l, h, p), h_fin


def gla_scan(q, k, v, g, s0):
    b, l, h, dk = q.shape
    dv = v.shape[-1]
    nc = l // GLA_CHUNK
    q = q.reshape(b, nc, GLA_CHUNK, h, dk)
    k = k.reshape(b, nc, GLA_CHUNK, h, dk)
    v = v.reshape(b, nc, GLA_CHUNK, h, dv)
    g = g.reshape(b, nc, GLA_CHUNK, h, dk)
    gc = jnp.cumsum(g, axis=2)
    g_last = gc[:, :, -1]
    q_in = q * jnp.exp(gc)
    k_in = k * jnp.exp(-gc)
    tri = jnp.tril(jnp.ones((GLA_CHUNK, GLA_CHUNK), dtype=bool))[None, None, None]
    att = jnp.where(tri, jnp.einsum('bcqhd,bcshd->bchqs', q_in, k_in), 0.0)
    o_intra = jnp.einsum('bchqs,bcshv->bcqhv', att, v)
    states = jnp.einsum('bcshd,bcshv->bchdv', k * jnp.exp(g_last[:, :, None] - gc), v)

    def step(carry, inp):
        st, dec = inp
        return carry * dec[..., None] + st, carry

    s_fin, s_enter = lax.scan(step, s0, (jnp.moveaxis(states, 1, 0), jnp.moveaxis(jnp.exp(g_last), 1, 0)))
    s_enter = jnp.moveaxis(s_enter, 0, 1)
    o_inter = jnp.einsum('bcqhd,bchdv->bcqhv', q_in, s_enter)
    return (o_intra + o_inter).reshape(b, l, h, dv), s_fin


def ssd_branch(z, xbc, dt, conv_w, conv_b, a_log, dt_bias, d_skip, norm_w, h0):
    b, l, _ = xbc.shape
    xbc = jax.nn.silu(depthwise_conv(xbc, conv_w, conv_b)).astype(jnp.float32)
    xs = xbc[..., :SSD_WIDTH].reshape(b, l, SSD_HEADS, SSD_HEAD_DIM)
    rep = SSD_HEADS // SSD_GROUPS
    bm = jnp.repeat(xbc[..., SSD_WIDTH:SSD_WIDTH + SSD_BC].reshape(b, l, SSD_GROUPS, SSD_STATE), rep, axis=2)
    cm = jnp.repeat(xbc[..., SSD_WIDTH + SSD_BC:].reshape(b, l, SSD_GROUPS, SSD_STATE), rep, axis=2)
    dt = jax.nn.softplus(dt.reshape(b, l, 2, SSD_HEADS).astype(jnp.float32) + dt_bias.astype(jnp.float32))
    a = -jnp.exp(a_log.astype(jnp.float32))
    y_f, s_f = ssd_scan(xs * dt[:, :, 0, :, None], dt[:, :, 0] * a[0], bm, cm, h0[:, 0])
    y_b, s_b = ssd_scan(flip(xs * dt[:, :, 1, :, None]), flip(dt[:, :, 1] * a[1]), flip(bm), flip(cm), h0[:, 1])
    y = y_f + flip(y_b) + xs * d_skip.astype(jnp.float32)[:, None]
    y = y.reshape(b, l, SSD_WIDTH).astype(z.dtype)
    y = rms_norm(y * jax.nn.silu(z), norm_w)
    return y, jnp.stack([s_f, s_b], axis=1)


def gla_branch(q, k, v, g, gk_lr, gk_up, gk_b, norm_w, s0):
    b, l, _ = q.shape
    qh = q.reshape(b, l, GLA_HEADS, GLA_DK).astype(jnp.float32) * (GLA_DK ** -0.5)
    kh = k.reshape(b, l, GLA_HEADS, GLA_DK).astype(jnp.float32)
    vh = v.reshape(b, l, GLA_HEADS, GLA_DV).astype(jnp.float32)
    lr = gk_lr.reshape(b, l, 2, GLA_LOWRANK)
    gk = jnp.einsum('blzr,zrk->blzk', lr, gk_up) + gk_b
    gk = (jax.nn.log_sigmoid(gk.astype(jnp.float32)) / GLA_GATE_NORM).reshape(b, l, 2, GLA_HEADS, GLA_DK)
    o_f, s_f = gla_scan(qh, kh, vh, gk[:, :, 0], s0[:, 0])
    o_b, s_b = gla_scan(flip(qh), flip(kh), flip(vh), flip(gk[:, :, 1]), s0[:, 1])
    o = rms_norm(o_f + flip(o_b), norm_w).reshape(b, l, GLA_WIDTH).astype(g.dtype)
    return o * jax.nn.silu(g), jnp.stack([s_f, s_b], axis=1)


def attn_heads(aq, ak, av, q_norm, k_norm):
    b, l, _ = aq.shape
    qh = rms_norm(aq.reshape(b, l, ATT_KV_HEADS, ATT_GROUP, HEAD_DIM), q_norm)
    kh = rms_norm(ak.reshape(b, l, ATT_KV_HEADS, HEAD_DIM), k_norm)
    vh = av.reshape(b, l, ATT_KV_HEADS, HEAD_DIM)
    return qh, kh, vh


def axial_rope(x, row_pos, col_pos):
    half = HEAD_DIM // 2
    quarter = HEAD_DIM // 4
    inv = ROPE_THETA ** (-jnp.arange(quarter, dtype=jnp.float32) / quarter)
    bshape = (x.shape[1],) + (1,) * (x.ndim - 3) + (quarter,)

    def rot(xa, pos):
        ang = pos[:, None] * inv[None, :]
        cos = jnp.cos(ang).reshape(bshape).astype(x.dtype)
        sin = jnp.sin(ang).reshape(bshape).astype(x.dtype)
        x1, x2 = xa[..., :quarter], xa[..., quarter:]
        return jnp.concatenate([x1 * cos - x2 * sin, x1 * sin + x2 * cos], axis=-1)

    return jnp.concatenate([rot(x[..., :half], row_pos), rot(x[..., half:], col_pos)], axis=-1)


def sink_logits(sink, lead):
    s = sink.astype(jnp.float32).reshape(ATT_KV_HEADS, ATT_GROUP)
    return jnp.broadcast_to(s[:, :, None, None], lead + (ATT_KV_HEADS, ATT_GROUP, ATT_BLOCK, 1))


def ctx_attention(q, k, v, sink):
    b, l = q.shape[:2]
    nb = l // ATT_BLOCK
    qb = jnp.moveaxis(q.reshape(b, nb, ATT_BLOCK, ATT_KV_HEADS, ATT_GROUP, HEAD_DIM), 1, 0)
    scale = HEAD_DIM ** -0.5

    def one_block(qblk):
        s = jnp.einsum('bqkgd,bskd->bkgqs', qblk, k).astype(jnp.float32) * scale
        p = jax.nn.softmax(jnp.concatenate([sink_logits(sink, (b,)), s], axis=-1), axis=-1)
        return jnp.einsum('bkgqs,bskd->bqkgd', p[..., 1:].astype(v.dtype), v)

    out = lax.map(one_block, qb)
    return jnp.moveaxis(out, 0, 1).reshape(b, l, ATT_WIDTH)


def latent_attention(q, k, v, k_ctx, v_ctx, sink):
    b, l = q.shape[:2]
    nb = l // ATT_BLOCK
    scale = HEAD_DIM ** -0.5
    qb = q.reshape(b, nb, ATT_BLOCK, ATT_KV_HEADS, ATT_GROUP, HEAD_DIM)

    def windows(t):
        tb = t.reshape(b, nb, ATT_BLOCK, ATT_KV_HEADS, HEAD_DIM)
        tp = jnp.pad(tb, ((0, 0), (1, 1), (0, 0), (0, 0), (0, 0)))
        return jnp.concatenate([tp[:, :-2], tp[:, 1:-1], tp[:, 2:]], axis=2)

    k_win, v_win = windows(k), windows(v)
    q_pos = jnp.arange(l).reshape(nb, ATT_BLOCK)
    k_pos = (jnp.arange(nb)[:, None] - 1) * ATT_BLOCK + jnp.arange(3 * ATT_BLOCK)[None, :]
    kp = k_pos[:, None, :]
    valid = (jnp.abs(q_pos[:, :, None] - kp) <= WINDOW) & (kp >= 0) & (kp < l)
    s_loc = jnp.einsum('bnqkgd,bnskd->bnkgqs', qb, k_win).astype(jnp.float32) * scale
    s_loc = jnp.where(valid[None, :, None, None], s_loc, NEG_INF)
    s_ctx = jnp.einsum('bnqkgd,bskd->bnkgqs', qb, k_ctx).astype(jnp.float32) * scale
    p = jax.nn.softmax(jnp.concatenate([sink_logits(sink, (b, nb)), s_ctx, s_loc], axis=-1), axis=-1)
    lc = k_ctx.shape[1]
    o = (jnp.einsum('bnkgqs,bskd->bnqkgd', p[..., 1:1 + lc].astype(v.dtype), v_ctx)
         + jnp.einsum('bnkgqs,bnskd->bnqkgd', p[..., 1 + lc:].astype(v.dtype), v_win))
    return o.reshape(b, l, ATT_WIDTH)


def context_layer(x, c_ctx, lw):
    (w_ada, b_ada, norm_w, w_in, conv_w, conv_b, a_log, dt_bias, d_skip, ssd_norm_w,
     gk_up, gk_b, gla_norm_w, q_norm, k_norm, sink, w_out) = lw
    b = x.shape[0]
    shift, scale, gate = modulation(c_ctx, w_ada, b_ada)
    h = rms_norm(x, norm_w) * (1 + scale) + shift
    z, xbc, dt, gq, gk, gv, gg, glr, aq, ak, av, ag = split_proj(h @ w_in)
    h0 = jnp.zeros((b, 2, SSD_HEADS, SSD_HEAD_DIM, SSD_STATE), jnp.float32)
    s0 = jnp.zeros((b, 2, GLA_HEADS, GLA_DK, GLA_DV), jnp.float32)
    y_ssd, st_ssd = ssd_branch(z, xbc, dt, conv_w, conv_b, a_log, dt_bias, d_skip, ssd_norm_w, h0)
    y_gla, st_gla = gla_branch(gq, gk, gv, gg, glr, gk_up, gk_b, gla_norm_w, s0)
    qh, kh, vh = attn_heads(aq, ak, av, q_norm, k_norm)
    y_att = ctx_attention(qh, kh, vh, sink) * jax.nn.silu(ag)
    y = jnp.concatenate([y_ssd, y_gla, y_att], axis=-1) @ w_out
    return x + gate * y, kh, vh, st_ssd, st_gla


def latent_layer(x, c, row_pos, col_pos, k_ctx, v_ctx, ssd_h0, gla_s0, lw):
    (w_ada, b_ada, norm_w, w_in, conv_w, conv_b, a_log, dt_bias, d_skip, ssd_norm_w,
     gk_up, gk_b, gla_norm_w, q_norm, k_norm, sink, w_out) = lw
    shift, scale, gate = modulation(c, w_ada, b_ada)
    h = rms_norm(x, norm_w) * (1 + scale[:, None]) + shift[:, None]
    z, xbc, dt, gq, gk, gv, gg, glr, aq, ak, av, ag = split_proj(h @ w_in)
    y_ssd, _ = ssd_branch(z, xbc, dt, conv_w, conv_b, a_log, dt_bias, d_skip, ssd_norm_w,
                          ssd_h0.astype(jnp.float32))
    y_gla, _ = gla_branch(gq, gk, gv, gg, glr, gk_up, gk_b, gla_norm_w, gla_s0.astype(jnp.float32))
    qh, kh, vh = attn_heads(aq, ak, av, q_norm, k_norm)
    qh = axial_rope(qh, row_pos, col_pos)
    kh = axial_rope(kh, row_pos, col_pos)
    y_att = latent_attention(qh, kh, vh, k_ctx, v_ctx, sink) * jax.nn.silu(ag)
    y = jnp.concatenate([y_ssd, y_gla, y_att], axis=-1) @ w_out
    return x + gate[:, None] * y


def setup_inputs(seed: int = 0) -> dict:
    key = jax.random.key(seed)
    ks = jax.random.split(key, 26)

    def nrm(k, shape, s):
        return jax.random.normal(k, shape, jnp.float32) * s

    x_prompt = nrm(ks[0], (BATCH, SEQ, D_MODEL), 1.0)
    x_sample = nrm(ks[1], (DEC_BATCH, DEC_SEQ, D_MODEL), 1.0)
    c = nrm(ks[2], (DEC_BATCH, D_MODEL), 1.0)
    cache_k = nrm(ks[3], (DEC_BATCH, DEPTH, PAST_LEN, ATT_KV_HEADS, HEAD_DIM), 1.0)
    cache_v = nrm(ks[4], (DEC_BATCH, DEPTH, PAST_LEN, ATT_KV_HEADS, HEAD_DIM), 1.0)
    state_ssd = nrm(ks[5], (DEC_BATCH, DEPTH, 2, SSD_HEADS, SSD_HEAD_DIM, SSD_STATE), 0.1)
    state_gla = nrm(ks[6], (DEC_BATCH, DEPTH, 2, GLA_HEADS, GLA_DK, GLA_DV), 0.1)
    c_ctx = nrm(ks[7], (D_MODEL,), 1.0)
    w_ada = nrm(ks[8], (DEPTH, D_MODEL, 3 * D_MODEL), 0.5 * D_MODEL ** -0.5)
    b_ada = nrm(ks[9], (DEPTH, 3 * D_MODEL), 0.02)
    norm_w = 1.0 + nrm(ks[10], (DEPTH, D_MODEL), 0.01)
    w_in = nrm(ks[11], (DEPTH, D_MODEL, D_IN), D_MODEL ** -0.5)
    conv_w = nrm(ks[12], (DEPTH, SSD_CONV, SSD_CONV_DIM), SSD_CONV ** -0.5)
    conv_b = nrm(ks[13], (DEPTH, SSD_CONV_DIM), 0.02)
    ssd_a_log = jnp.log(jax.random.uniform(ks[14], (DEPTH, 2, SSD_HEADS), jnp.float32, 1.0, 16.0))
    dt0 = jnp.exp(jax.random.uniform(ks[15], (DEPTH, 2, SSD_HEADS), jnp.float32,
                                     math.log(1e-3), math.log(1e-1)))
    ssd_dt_bias = dt0 + jnp.log(-jnp.expm1(-dt0))
    ssd_d = 1.0 + nrm(ks[16], (DEPTH, SSD_HEADS), 0.1)
    ssd_norm_w = 1.0 + nrm(ks[17], (DEPTH, SSD_WIDTH), 0.01)
    gla_gk_up = nrm(ks[18], (DEPTH, 2, GLA_LOWRANK, GLA_KEY_DIM), GLA_LOWRANK ** -0.5)
    gla_gk_b = nrm(ks[19], (DEPTH, 2, GLA_KEY_DIM), 0.1)
    gla_norm_w = 1.0 + nrm(ks[20], (DEPTH, GLA_DV), 0.01)
    attn_q_norm = 1.0 + nrm(ks[21], (DEPTH, HEAD_DIM), 0.01)
    attn_k_norm = 1.0 + nrm(ks[22], (DEPTH, HEAD_DIM), 0.01)
    attn_sink = nrm(ks[23], (DEPTH, ATT_HEADS), 1.0)
    w_out = nrm(ks[24], (DEPTH, D_MIX, D_MODEL), D_MIX ** -0.5)
    return {"x_prompt": x_prompt, "x_sample": x_sample, "c": c,
            "cache_k": cache_k, "cache_v": cache_v, "state_ssd": state_ssd, "state_gla": state_gla,
            "c_ctx": c_ctx, "w_ada": w_ada, "b_ada": b_ada, "norm_w": norm_w, "w_in": w_in,
            "conv_w": conv_w, "conv_b": conv_b, "ssd_a_log": ssd_a_log, "ssd_dt_bias": ssd_dt_bias,
            "ssd_d": ssd_d, "ssd_norm_w": ssd_norm_w, "gla_gk_up": gla_gk_up, "gla_gk_b": gla_gk_b,
            "gla_norm_w": gla_norm_w, "attn_q_norm": attn_q_norm, "attn_k_norm": attn_k_norm,
            "attn_sink": attn_sink, "w_out": w_out}


def reference(x_prompt, x_sample, c, cache_k, cache_v, state_ssd, state_gla, c_ctx, w_ada, b_ada,
              norm_w, w_in, conv_w, conv_b, ssd_a_log, ssd_dt_bias, ssd_d, ssd_norm_w, gla_gk_up,
              gla_gk_b, gla_norm_w, attn_q_norm, attn_k_norm, attn_sink, w_out):
    rows = x_sample.shape[1] // GRID_W
    row_pos = jnp.repeat(jnp.arange(rows, dtype=jnp.float32), GRID_W)
    col_pos = jnp.tile(jnp.arange(GRID_W, dtype=jnp.float32), rows)
    x_ctx, x_lat = x_prompt, x_sample
    ks_out, vs_out, ssd_out, gla_out = [], [], [], []
    for l in range(DEPTH):
        lw = (w_ada[l], b_ada[l], norm_w[l], w_in[l], conv_w[l], conv_b[l], ssd_a_log[l], ssd_dt_bias[l],
              ssd_d[l], ssd_norm_w[l], gla_gk_up[l], gla_gk_b[l], gla_norm_w[l], attn_q_norm[l],
              attn_k_norm[l], attn_sink[l], w_out[l])
        x_ctx, k_l, v_l, s_ssd_l, s_gla_l = context_layer(x_ctx, c_ctx, lw)
        ks_out.append(k_l)
        vs_out.append(v_l)
        ssd_out.append(s_ssd_l)
        gla_out.append(s_gla_l)
        x_lat = latent_layer(x_lat, c, row_pos, col_pos, cache_k[:, l], cache_v[:, l],
                             state_ssd[:, l], state_gla[:, l], lw)
    new_cache_k = jnp.stack(ks_out, axis=1)
    new_cache_v = jnp.stack(vs_out, axis=1)
    new_state_ssd = jnp.stack(ssd_out, axis=1)
    new_state_gla = jnp.stack(gla_out, axis=1)
    return (x_ctx, x_lat, new_cache_k, new_cache_v, new_state_ssd, new_state_gla)
```

```python
import numpy as np
import concourse.bass as bass
import concourse.mybir as mybir
from concourse.bass_utils import run_bass_kernel_spmd

F32 = mybir.dt.float32
BF16 = mybir.dt.bfloat16
AF = mybir.ActivationFunctionType
ALU = mybir.AluOpType
AX = mybir.AxisListType

EPS = 1e-6
D = 1024
DIN = 3120


class _Op:
    __slots__ = ("eng", "fn", "deps", "dma", "signal", "sigval", "dsem", "dval", "dprev")

    def __init__(self, eng, fn, deps, dma):
        self.eng = eng
        self.fn = fn
        self.deps = deps
        self.dma = dma
        self.signal = False
        self.sigval = 0
        self.dsem = None
        self.dval = 0
        self.dprev = 0


class Sched:
    ENGS = ("pe", "act", "dve", "pool", "sp")
    LIM = 30000

    def __init__(self, nc, n_dma_sems=12):
        self.nc = nc
        self.ops = []
        self.writers = {}
        self.readers = {}
        self.K = n_dma_sems
        self.strict = True
        self.attach_wait = True
        self.eng_free = {e: 0.0 for e in self.ENGS}
        self.key_wr = {}
        self.key_rd = {}
        self.last_end = 0.0

    def add(self, eng, fn, r=(), w=(), dma=False, cost=300.0):
        idx = len(self.ops)
        ops = self.ops
        r = list(dict.fromkeys(r))
        w = list(dict.fromkeys(w))
        st = self.eng_free[eng]
        for k in r:
            st = max(st, self.key_wr.get(k, 0.0))
            if isinstance(k, str) and k[:2] in ("fb", "tb"):
                st = max(st, self.key_rd.get(k, 0.0))
        for k in w:
            st = max(st, self.key_wr.get(k, 0.0), self.key_rd.get(k, 0.0))
        en = st + cost
        self.eng_free[eng] = st + 60.0 if dma else en
        for k in w:
            self.key_wr[k] = en
        for k in r:
            self.key_rd[k] = max(self.key_rd.get(k, 0.0), en)
        self.last_end = en
        deps = {}
        for k in r:
            for p in self.writers.get(k, ()):
                deps[p] = True
            if isinstance(k, str) and k[:2] in ("fb", "tb"):
                for p in self.readers.get(k, ()):
                    if ops[p].eng != eng:
                        deps.setdefault(p, False)
        for k in w:
            rd = self.readers.get(k)
            wr = self.writers.get(k)
            if rd:
                for p in rd:
                    deps.setdefault(p, False)
                if wr:
                    for p in wr:
                        deps.setdefault(p, False)
                self.writers[k] = [idx]
                self.readers[k] = []
            elif wr:
                for p in wr:
                    deps.setdefault(p, False)
                nw = [p for p in wr if ops[p].dma or dma or ops[p].eng != eng]
                nw.append(idx)
                self.writers[k] = nw
            else:
                self.writers[k] = [idx]
        for k in r:
            lst = self.readers.setdefault(k, [])
            if not dma:
                lst[:] = [p for p in lst if ops[p].dma or ops[p].eng != eng]
            lst.append(idx)
        deps.pop(idx, None)
        ops.append(_Op(eng, fn, deps, dma))
        return idx

    def emit(self):
        nc = self.nc
        ops = self.ops
        for op in ops:
            nd = {}
            for p, raw in op.deps.items():
                P = ops[p]
                if P.dma:
                    nd[p] = raw
                    continue
                if P.eng == op.eng and not op.dma:
                    if op.eng == "pe" or (not raw and not self.strict):
                        continue
                nd[p] = raw
                P.signal = True
            op.deps = nd
        cnt = {e: 0 for e in self.ENGS}
        dcnt = {e: 0 for e in self.ENGS}
        for op in ops:
            if op.dma:
                n = dcnt[op.eng]
                dcnt[op.eng] = n + 1
                op.dsem = (op.eng, n % self.K)
                op.dval = 16 * (n // self.K + 1)
                op.dprev = 16 * (n // self.K)
            elif op.signal:
                cnt[op.eng] += 1
                op.sigval = cnt[op.eng]
        sems = {}
        for e in self.ENGS:
            for ep in range((cnt[e] + self.LIM - 1) // self.LIM):
                sems[(e, ep)] = nc.alloc_semaphore(f"s_{e}_{ep}")
        dsems = {}
        for e in self.ENGS:
            for j in range(min(self.K, dcnt[e])):
                dsems[(e, j)] = nc.alloc_semaphore(f"d_{e}_{j}")
        LIM = self.LIM

        def sig_of(P):
            ep = (P.sigval - 1) // LIM
            return (P.eng, ep), P.sigval - ep * LIM

        per_eng = {e: [op for op in ops if op.eng == e] for e in self.ENGS}

        def run(ename, eh):
            waited = {}
            dwaited = {}
            for op in per_eng[ename]:
                need = {}
                dneed = {}
                for p in op.deps:
                    P = ops[p]
                    if P.dma:
                        if dwaited.get(P.dsem, 0) < P.dval:
                            dneed[P.dsem] = max(dneed.get(P.dsem, 0), P.dval)
                    else:
                        skey, v = sig_of(P)
                        if waited.get(skey, 0) < v:
                            need[skey] = max(need.get(skey, 0), v)
                if op.dma and op.dprev > 0 and dwaited.get(op.dsem, 0) < op.dprev:
                    dneed[op.dsem] = max(dneed.get(op.dsem, 0), op.dprev)
                wl = [(sems[skey], v) for skey, v in need.items()] + [(dsems[skey], v) for skey, v in dneed.items()]
                for skey, v in need.items():
                    waited[skey] = v
                for skey, v in dneed.items():
                    dwaited[skey] = v
                attach = wl.pop() if (wl and self.attach_wait) else None
                for sm, v in wl:
                    eh.wait_ge(sm, v)
                ins = op.fn(eh)
                if attach is not None:
                    ins._wait_ge(attach[0], attach[1])
                if op.dma:
                    ins.then_inc(dsems[op.dsem], 16)
                elif op.signal:
                    skey, _ = sig_of(op)
                    ins.then_inc(sems[skey], 1)
            for j in range(min(self.K, dcnt[ename])):
                n_on = (dcnt[ename] - 1 - j) // self.K + 1
                if dwaited.get((ename, j), 0) < 16 * n_on:
                    eh.wait_ge(dsems[(ename, j)], 16 * n_on)

        with nc.Block() as block:
            @block.sync
            def _(e):
                run("sp", e)

            @block.tensor
            def _(e):
                run("pe", e)

            @block.scalar
            def _(e):
                run("act", e)

            @block.vector
            def _(e):
                run("dve", e)

            @block.gpsimd
            def _(e):
                run("pool", e)


class _Pool:
    def __init__(self, tiles, names):
        self.tiles = tiles
        self.names = names
        self.free = list(range(len(tiles)))

    def alloc(self):
        assert self.free, "PSUM bank pool exhausted"
        i = self.free.pop(0)
        return i

    def release(self, i):
        assert i not in self.free
        self.free.append(i)


C_XBC, C_GQ, C_GK, C_LR, C_Z, C_B2, C_B3, C_AQ, C_KV, NCOL = 0, 1024, 1152, 1280, 1312, 1824, 2224, 2736, 2992, 3248
WIN_COPIES = [(0, 512, 1024), (1024, 1552, 128), (1152, 1680, 128), (1280, 2320, 32), (1312, 0, 512),
              (1824, 1536, 16), (1840, 1680, 384), (2224, 2064, 256), (2480, 2864, 256),
              (2736, 2352, 256), (2992, 2608, 256)]
K_ID, K_LE, K_GE, K_GT, K_LT, K_ONE, K_BLK4, K_BLKS, NCONST = 0, 128, 256, 384, 512, 640, 768, 1280, 1536


def build_consts():
    c = np.zeros((128, NCONST), np.float32)
    k = np.arange(128)[:, None]
    t = np.arange(128)[None, :]
    c[:, K_ID:K_ID + 128] = (k == t)
    c[:, K_LE:K_LE + 128] = (k <= t)
    c[:, K_GE:K_GE + 128] = (k >= t)
    c[:, K_GT:K_GT + 128] = (k > t)
    c[:, K_LT:K_LT + 128] = (k < t)
    c[:, K_ONE:K_ONE + 128] = 1.0
    blk4 = np.zeros((128, 4, 128), np.float32)
    for h in range(4):
        blk4[32 * h:32 * h + 32, h, :] = 1.0
    c[:, K_BLK4:K_BLK4 + 512] = blk4.reshape(128, 512)
    blks = np.zeros((128, 4, 64), np.float32)
    for h in range(4):
        blks[32 * h:32 * h + 32, h, :] = 1.0
    c[:, K_BLKS:K_BLKS + 256] = blks.reshape(128, 256)
    return c


def build_rope(LS):
    rows = LS // 64
    row_pos = np.repeat(np.arange(rows, dtype=np.float32), 64)
    col_pos = np.tile(np.arange(64, dtype=np.float32), rows)
    inv = (10000.0 ** (-np.arange(16, dtype=np.float32) / 16)).astype(np.float32)
    ar = (row_pos[:, None] * inv[None, :]).astype(np.float32)
    ac = (col_pos[:, None] * inv[None, :]).astype(np.float32)
    cr, sr, cc, sc = np.cos(ar), np.sin(ar), np.cos(ac), np.sin(ac)
    tab = np.concatenate([cr, cr, cc, cc, -sr, sr, -sc, sc], axis=1).astype(np.float32)
    return tab


def build_program(LS, NPR, NL, dump=None):
    LP = 256
    TS = 256
    nc = bass.Bass("TRN2", target_bir_lowering=False)
    S = Sched(nc)
    dump = dump or {}

    def din(name, shape, dt=F32):
        return nc.dram_tensor(name, list(shape), dt, kind="ExternalInput").ap()

    def dout(name, shape, dt=F32):
        return nc.dram_tensor(name, list(shape), dt, kind="ExternalOutput").ap()

    xs_d = din("xs", [LS, D])
    xp_d = din("xp", [NPR * LP, D])
    cvec_d = din("cvec", [2, D])
    ck_d = din("ck", [NL, 256, 128])
    cv_d = din("cv", [NL, 256, 128])
    sssd_d = din("sssd", [NL, 2, 512, 128])
    sgla_d = din("sgla", [NL, 2, 128, 64])
    rope_d = din("rope", [LS, 128])
    consts_d = din("consts", [128, NCONST])
    w_ada_d = din("w_ada", [NL, D, 3 * D])
    b_ada_d = din("b_ada", [NL, 3 * D])
    norm_w_d = din("norm_w", [NL, D])
    w_in_d = din("w_in", [NL, D, DIN])
    conv_w_d = din("conv_w", [NL, 5, 1024])
    conv_b_d = din("conv_b", [NL, 1024])
    a_log_d = din("a_log", [NL, 16])
    dt_bias_d = din("dt_bias", [NL, 16])
    ssd_d_d = din("ssd_d", [NL, 8])
    ssd_nw_d = din("ssd_norm_w", [NL, 512])
    gk_up_d = din("gk_up", [NL, 2, 16, 128])
    gk_b_d = din("gk_b", [NL, 256])
    gla_nw_d = din("gla_norm_w", [NL, 64])
    q_norm_d = din("q_norm", [NL, 64])
    k_norm_d = din("k_norm", [NL, 64])
    sink_d = din("sink", [NL, 4])
    w_out_d = din("w_out", [NL, D, D])

    ys_d = dout("ys", [LS, D])
    yp_d = dout("yp", [NPR * LP, D])
    nk_d = dout("nk", [NPR, NL, LP, 128])
    nv_d = dout("nv", [NPR, NL, LP, 128])
    nssd_d = dout("nssd", [NPR, NL, 2, 512, 128])
    ngla_d = dout("ngla", [NPR, NL, 2, 128, 64])
    NCH_S = LS // 128
    hb_scr = nc.dram_tensor("hb_scr", [NCH_S, 128, 512], BF16, kind="Internal").ap()
    sb_scr = nc.dram_tensor("sb_scr", [NCH_S, 128, 256], BF16, kind="Internal").ap()
    ht_scr = nc.dram_tensor("ht_scr", [max(NCH_S // 2, 1), 128, 8 * 260], BF16, kind="Internal").ap()
    xc_scr = nc.dram_tensor("xc_scr", [max(NCH_S // 2, 1), 128, 8 * 256], BF16, kind="Internal").ap()
    sz_scr = nc.dram_tensor("sz_scr", [max(NCH_S, 2), 128, 512], BF16, kind="Internal").ap()
    sg_scr = nc.dram_tensor("sg_scr", [max(NCH_S, 2), 128, 512], BF16, kind="Internal").ap()
    lr_scr = nc.dram_tensor("lr_scr", [max(NCH_S // 2, 1), 32, 256], F32, kind="Internal").ap()
    kt_scr = nc.dram_tensor("kt_scr", [max(NCH_S, 2), 128, 128], F32, kind="Internal").ap()
    vt_scr = nc.dram_tensor("vt_scr", [max(NCH_S, 2), 128, 256], BF16, kind="Internal").ap()
    dx_scr = nc.dram_tensor("dx_scr", [max(NCH_S, 2), 128, 16], F32, kind="Internal").ap()
    bt_scr = nc.dram_tensor("bt_scr", [max(NCH_S, 2), 128, 256], BF16, kind="Internal").ap()
    dump_d = {n: dout("dbg_" + n, shp) for n, shp in dump.items()}

    def fsz(ap):
        n = 1
        for d in ap.shape[1:]:
            n *= d
        return n

    def vcost(eng, ap):
        n = fsz(ap)
        if eng == "pool":
            return 250.0 + 1.8 * n
        if eng == "act":
            return 280.0 + 0.45 * n
        return 130.0 + 0.85 * n

    def MM(out, lhsT, rhs, start=True, stop=True, r=(), w=()):
        f = 4.0 if lhsT.dtype == F32 else 1.0
        S.add("pe", lambda e: e.matmul(out, lhsT=lhsT, rhs=rhs, start=start, stop=stop), r, w,
              cost=max(fsz(out), 64) / 2.4 * f + 60.0)

    def TR(out, in_, ident, r=(), w=()):
        f = 4.0 if in_.dtype == F32 else 1.0
        S.add("pe", lambda e: e.transpose(out=out, in_=in_, identity=ident), r, w, cost=128 / 2.4 * f + 12.0)

    def ACT(out, in_, func, r=(), w=(), **kw):
        S.add("act", lambda e: e.activation(out=out, in_=in_, func=func, **kw), r, w, cost=vcost("act", out))

    def TT(eng, out, in0, in1, op, r=(), w=()):
        S.add(eng, lambda e: e.tensor_tensor(out=out, in0=in0, in1=in1, op=op), r, w, cost=vcost(eng, out))

    def TS_(eng, out, in0, s1, s2, op0, op1=None, r=(), w=()):
        if op1 is None:
            S.add(eng, lambda e: e.tensor_scalar(out=out, in0=in0, scalar1=s1, scalar2=None, op0=op0), r, w, cost=vcost(eng, out))
        else:
            S.add(eng, lambda e: e.tensor_scalar(out=out, in0=in0, scalar1=s1, scalar2=s2, op0=op0, op1=op1), r, w, cost=vcost(eng, out))

    def STT(eng, out, in0, scalar, in1, op0, op1, r=(), w=()):
        S.add(eng, lambda e: e.scalar_tensor_tensor(out=out, in0=in0, scalar=scalar, in1=in1, op0=op0, op1=op1), r, w, cost=vcost(eng, out))

    def CP(eng, out, in_, r=(), w=()):
        if eng == "act":
            S.add("act", lambda e: e.activation(out=out, in_=in_, func=AF.Copy), r, w, cost=vcost("act", out))
        else:
            S.add(eng, lambda e: e.tensor_copy(out=out, in_=in_), r, w, cost=vcost(eng, out))

    def MSET(eng, ap, val, w=()):
        S.add(eng, lambda e: e.memset(ap, val), (), w)

    def DMA(out, in_, r=(), w=(), q="sp", slow=False):
        c_ = 2000.0 + fsz(out) * 4.0 * 128 / 150.0
        if slow:
            S.add(q, lambda e: e.dma_start(out=out, in_=in_, allow_slow_non_contiguous=True), r, w, dma=True, cost=c_)
        else:
            S.add(q, lambda e: e.dma_start(out=out, in_=in_), r, w, dma=True, cost=c_)

    def DUMP(name, ap, r):
        if name in dump_d:
            DMA(dump_d[name], ap, r=r, w=["dbg_" + name])

    A = nc.alloc_sbuf_tensor

    def rstd_ops(ss_ap, out_ap, n, r, w):
        ACT(out_ap, ss_ap, AF.Ln, r=r, w=w, scale=1.0 / n, bias=EPS)
        ACT(out_ap, out_ap, AF.Exp, r=w, w=w, scale=-0.5)

    fbank = [nc.alloc_psum_tensor(f"fb{i}", [128, 512], F32) for i in range(6)]
    tbank = [nc.alloc_psum_tensor(f"tb{i}", [128, 1024], BF16) for i in range(2)]
    FB = _Pool(fbank, [f"fb{i}" for i in range(6)])
    TB = _Pool(tbank, [f"tb{i}" for i in range(2)])

    cst = A("cst", [128, 1024], F32)
    cstb = A("cstb", [128, 1152], BF16)
    KB_BLK4 = 640
    KF_BLKS = 768
    identf = cst[:, K_ID:K_ID + 128]
    identb = cstb[:, K_ID:K_ID + 128]
    onesf = cst[:, K_ONE:K_ONE + 128]

    def mf(c0):
        return cst[:, c0:c0 + 128]

    def mb(c0):
        return cstb[:, c0:c0 + 128]

    win = A("win", [128, 8, NCOL], BF16)
    wout = A("wout", [128, 8, D], BF16)
    gate_bc = A("gate_bc", [128, 2, D], BF16)
    scT_t = A("scT", [128, 16], F32)
    scT = scT_t[:, :].rearrange("p (c k) -> p c k", c=2)
    modT = A("modT", [128, 24, 2], F32)
    prow = A("prow", [128, 128], F32)
    colsT = A("colsT", [128, 128], F32)
    badaT = colsT[:, 0:24]
    dg = A("dg", [128, 128], F32)
    normwT = colsT[:, 24:32]
    g1T = A("g1T", [128, 8, 2], F32)
    shT = A("shT", [128, 8, 2], F32)
    cwT = colsT[:, 32:72].rearrange("p (t g) -> p t g", t=5)
    cbT = colsT[:, 72:80]
    A_bc = A("A_bc", [128, 16], F32)
    dtb_bc = A("dtb_bc", [128, 16], F32)
    D_bc = A("D_bc", [128, 8], F32)
    rsT = A("rsT", [128, 8], F32)
    qkw_bc = A("qkw_bc", [128, 6, 64], F32)
    gkup = A("gkup", [33, 256], F32)
    esink = A("esink", [128, 4], F32)

    xtH_t = A("xtH", [128, 2, D], F32)
    xtH = [xtH_t[:, 0, :], xtH_t[:, 1, :]]
    stage = xtH_t[:, :, :].rearrange("p j d -> p (j d)")
    STG = ["xtH0", "xtH1"]
    xtO = [A(f"xtO{i}", [128, D], F32) for i in range(2)]
    DMA(stage[:, 0:NCONST], consts_d, w=STG)
    CP("dve", cst[:, 0:768], stage[:, 0:768], r=STG, w=["cst"])
    CP("dve", cst[:, 768:1024], stage[:, K_BLKS:K_BLKS + 256], r=STG, w=["cst"])
    CP("dve", cstb[:, 0:640], stage[:, 0:640], r=STG, w=["cstb"])
    CP("dve", cstb[:, 640:1152], stage[:, K_BLK4:K_BLK4 + 512], r=STG, w=["cstb"])
    MSET("dve", prow[:], 0.0, w=["prow"])
    for c in range(2):
        DMA(prow[c * 8:(c + 1) * 8, :], cvec_d[c].rearrange("(k p) -> k p", p=128), w=["prow"])
    pb0 = FB.alloc()
    TR(fbank[pb0][:, 0:128], prow[:], identf, r=["prow", "cst"], w=[FB.names[pb0]])
    ACT(scT_t[:, :], fbank[pb0][:, 0:16], AF.Silu, r=[FB.names[pb0]], w=["scT"])
    FB.release(pb0)
    MSET("dve", rsT[:], 1.0, w=["rsT"])
    MSET("dve", gkup[:], 0.0, w=["gkup"])

    hT = [A(f"hT{i}", [128, 8, TS + 4], BF16) for i in range(2)]
    xh = A("xh", [128, D], BF16)
    ss2 = A("ss2", [128, 2], F32)
    rs2 = A("rs2", [128, 2], F32)
    xpre = A("xpre", [128, 3, TS + 4], BF16)
    cdiag = A("cdiag", [128, 5, 8, 128], BF16)
    xc = A("xc", [128, 8, TS], BF16)
    gqk = A("gqk", [128, 2, TS], F32)
    lrT = A("lrT", [33, TS], F32)
    MSET("dve", lrT[32:33, :], 1.0, w=["lrT1"])
    sz = [A(f"sz{i}", [128, 512], BF16) for i in range(2)]
    sg = [A(f"sg{i}", [128, 512], BF16) for i in range(2)]
    dtx = [A(f"dtx{i}", [128, 16], F32) for i in range(2)]
    dte = A("dte", [128, 16], F32)
    dtt = A("dtt", [128, 16], F32)
    aa = A("aa", [128, 16], F32)
    dum = A("dum", [128, 1], F32)
    ecs = A("ecs", [128, 32], F32)
    mdt = A("mdt", [128, 8], F32)
    ktok = [A(f"ktok{i}", [128, 128], F32) for i in range(2)]
    vtok = [A(f"vtok{i}", [128, 256], BF16) for i in range(2)]
    Lh = A("Lh", [128, 16, 128], BF16)
    eseg = A("eseg", [128, 16, 128], BF16)
    GM = A("GM", [128, 2, 2, 128], BF16)
    xdt = A("xdt", [128, 2, 512], BF16)
    xdec = A("xdec", [128, 512], BF16)
    diagD = A("diagD", [128, 4, 128], BF16)
    Dcol = A("Dcol", [128, 4], F32)
    Btok = A("Btok", [128, 256], BF16)
    Hf = A("Hf", [128, 512], F32)
    Hb = A("Hb", [128, 512], F32)
    Hfb = A("Hfb", [128, 512], BF16)
    Hio = A("Hio", [128, 512], BF16)
    ysb = A("ysb", [128, 512], F32)
    sti = ysb[:, :].rearrange("p (b n) -> p b n", b=4)
    yt2 = A("yt2", [128, 512], F32)
    xo = A("xo", [128, 512], F32)
    ssy = A("ssy", [128, 1], F32)
    rsy = A("rsy", [128, 1], F32)
    ymix = [A(f"ymix{i}", [128, D], BF16) for i in range(2)]
    ymT = A("ymT", [128, 8, 128], BF16)
    sp_ = A("sp_", [128, 256], F32)
    egq = A("egq", [128, 2, 128], F32)
    egk = A("egk", [128, 2, 128], F32)
    ekd = A("ekd", [128, 128], F32)
    egt = A("egt", [128, 2], F32)
    qin = A("qin", [128, 2, 128], BF16)
    kin = A("kin", [128, 2, 128], BF16)
    Qblk = A("Qblk", [128, 2, 512], BF16)
    attm = A("attm", [128, 2, 512], BF16)
    kdec = A("kdec", [128, 128], BF16)
    Sf = A("Sf", [128, 256], F32)
    Sb = A("Sb", [128, 256], F32)
    Sfb = A("Sfb", [128, 256], BF16)
    Sio = A("Sio", [128, 256], BF16)
    stmp = A("stmp", [128, 256], F32)
    sso = A("sso", [128, 4], F32)
    rso = A("rso", [128, 4], F32)
    og = A("og", [128, 256], F32)
    oat = A("oat", [128, 256], F32)
    ckst = og[:, :].rearrange("p (b f) -> p b f", b=2)
    ropet = [A(f"ropet{i}", [128, 128], F32) for i in range(2)]
    ssq = A("ssq", [128, 4], F32)
    rsq = A("rsq", [128, 4], F32)
    sj = A("sj", [128, 512], BF16)
    qn = A("qn", [128, 256], F32)
    raw = [A(f"raw{i}", [128, 256], F32) for i in range(2)]
    qr1 = A("qr1", [128, 256], F32)
    qr2 = A("qr2", [128, 256], F32)
    qb = A("qb", [128, 256], BF16)
    qT = A("qT", [128, 2, 128], BF16)
    knb = A("knb", [128, 128], BF16)
    NCHMAX = max(LS, LP) // 128
    KT = A("KT", [128, NCHMAX * 128], BF16)
    Vt = A("Vt", [128, NCHMAX, 2, 65], BF16)
    KTc = A("KTc", [128, 256], BF16)
    Vc = A("Vc", [128, 2, 2, 65], BF16)
    ckb = A("ckb", [128, 2, 128], BF16)
    PT = A("PT", [128, 5, 256], BF16)
    den = A("den", [128, 4], F32)
    MSET("dve", Vt[:, :, :, 64:65], 1.0, w=["Vt1"])
    MSET("dve", Vc[:, :, :, 64:65], 1.0, w=["Vc1"])

    def galloc(pool):
        while not pool.free:
            yield "blocked"
        return pool.alloc()

    def rr(gens):
        gens = [[g, 0.0] for g in gens if g is not None]
        blocked = set()
        nblocked = 0
        while gens:
            cand = [x for x in gens if id(x) not in blocked]
            if not cand:
                blocked.clear()
                cand = gens
            x = min(cand, key=lambda y: y[1])
            try:
                v = next(x[0])
            except StopIteration:
                gens.remove(x)
                blocked.clear()
                nblocked = 0
                continue
            if v == "blocked":
                blocked.add(id(x))
                nblocked += 1
                if nblocked > 4 * len(gens) + 4:
                    raise RuntimeError("emission schedule deadlock (PSUM pool / flags)")
            else:
                x[1] = S.last_end
                blocked.clear()
                nblocked = 0

    def chain(*gs):
        for g in gs:
            if g is not None:
                yield from g

    def lane(flags, key, g):
        yield from g
        flags[key] = flags.get(key, 0) + 1

    def gated(flags, key, need, g):
        while flags.get(key, 0) < need:
            yield "blocked"
        yield from g

    def run1(g):
        rr([g])

    def load_layer(l):
        stg = [(xtH[0], "xtH0"), (xtH[1], "xtH1"), (xtO[0][:, :], "xtO0"), (xtO[1][:, :], "xtO1")]
        cnt = [0]

        def nxt():
            t_ = stg[cnt[0] % 4]
            cnt[0] += 1
            return t_
        DMA(prow[0:24, :], b_ada_d[l].rearrange("(j p) -> j p", p=128), w=["prow"])
        DMA(prow[24:32, :], norm_w_d[l].rearrange("(j p) -> j p", p=128), w=["prow"])
        DMA(prow[32:72, :], conv_w_d[l].rearrange("t (g p) -> (t g) p", p=128), w=["prow"])
        DMA(prow[72:80, :], conv_b_d[l].rearrange("(j p) -> j p", p=128), w=["prow"])
        DMA(prow[80:84, :], ssd_nw_d[l].rearrange("(j p) -> j p", p=128), w=["prow"])
        pbc = FB.alloc()
        TR(fbank[pbc][:, 0:128], prow[:], identf, r=["prow", "cst"], w=[FB.names[pbc]])
        CP("dve", colsT[:], fbank[pbc][:, 0:128], r=[FB.names[pbc]], w=["colsT"])
        FB.release(pbc)
        CP("dve", rsT[:, 0:4], colsT[:, 80:84], r=["colsT"], w=["rsT"])
        for t in range(5):
            for g in range(8):
                if (t * 8 + g) % 2 == 0:
                    TS_("dve", cdiag[:, t, g, :], identf, cwT[:, t, g:g + 1], None, ALU.mult, r=["cst", "colsT"], w=["cdiag"])
                else:
                    ACT(cdiag[:, t, g, :], identf, AF.Copy, r=["cst", "colsT"], w=["cdiag"], scale=cwT[:, t, g:g + 1])
        DMA(A_bc[:], a_log_d[l:l + 1, :].partition_broadcast(128), w=["A_bc"])
        ACT(A_bc[:], A_bc[:], AF.Exp, r=["A_bc"], w=["A_bc"])
        TS_("dve", A_bc[:], A_bc[:], -1.0, None, ALU.mult, r=["A_bc"], w=["A_bc"])
        DMA(dtb_bc[:], dt_bias_d[l:l + 1, :].partition_broadcast(128), w=["dtb_bc"])
        DMA(D_bc[:], ssd_d_d[l:l + 1, :].partition_broadcast(128), w=["D_bc"])
        dv_ = D_bc[:, :].rearrange("p (g two) -> p g two", two=2)
        CP("dve", Dcol[0:64, :], dv_[0:64, :, 0], r=["D_bc"], w=["Dcol"])
        CP("dve", Dcol[64:128, :], dv_[64:128, :, 1], r=["D_bc"], w=["Dcol"])
        for g in range(4):
            TS_("dve", diagD[:, g, :], identf, Dcol[:, g:g + 1], None, ALU.mult, r=["cst", "Dcol"], w=["diagD"])
        for half in range(2):
            for kk in (4, 5):
                DMA(rsT[half * 64:(half + 1) * 64, kk:kk + 1], gla_nw_d[l].rearrange("(p o) -> p o", o=1), w=["rsT"])
        for hh in range(6):
            src = q_norm_d if hh < 4 else k_norm_d
            DMA(qkw_bc[:, hh, :], src[l:l + 1, :].partition_broadcast(128), w=["qkw_bc"])
        DMA(gkup[0:16, 0:128], gk_up_d[l, 0], w=["gkup"])
        DMA(gkup[16:32, 128:256], gk_up_d[l, 1], w=["gkup"])
        DMA(gkup[32:33, :], gk_b_d[l:l + 1, :], w=["gkup"])
        DMA(esink[:], sink_d[l:l + 1, :].partition_broadcast(128), w=["esink"])
        ACT(esink[:], esink[:], AF.Exp, r=["esink"], w=["esink"])
        def emit_win(k, c_lo, c_hi):
            st_, sk_ = nxt()
            DMA(st_[:, 0:c_hi - c_lo], w_in_d[l, k * 128:(k + 1) * 128, c_lo:c_hi], w=[sk_])
            for ci, (dst, src, n) in enumerate(WIN_COPIES):
                if not (c_lo <= src and src + n <= c_hi):
                    continue
                eng = ("dve", "act", "dve", "act", "pool")[(ci + k) % 5]
                CP(eng, win[:, k, dst:dst + n], st_[:, src - c_lo:src - c_lo + n], r=[sk_], w=["win"])

        pm = FB.alloc()

        def emit_ada(c3, k):
            st_, sk_ = nxt()
            DMA(st_[:, 0:1024], w_ada_d[l, k * 128:(k + 1) * 128, c3 * 1024:(c3 + 1) * 1024], w=[sk_])
            for jj in range(8):
                jg = c3 * 8 + jj
                MM(fbank[pm][:, jg * 2:jg * 2 + 2], lhsT=st_[:, jj * 128:(jj + 1) * 128], rhs=scT[:, :, k],
                   start=(c3 == 0 and k == 0 and jj == 0), stop=(c3 == 2 and k == 7 and jj == 7),
                   r=[sk_, "scT"], w=[FB.names[pm]])

        win_pieces = [(k, lo, hi) for k in range(8) for (lo, hi) in ((0, 512), (512, 1536), (1536, 2352), (2352, DIN))]
        ada_pieces = [(c3, k) for c3 in range(3) for k in range(8)]
        for i_ in range(len(win_pieces)):
            emit_win(*win_pieces[i_])
            if i_ < len(ada_pieces):
                emit_ada(*ada_pieces[i_])
        CP("dve", modT[:].rearrange("p j c -> p (j c)"), fbank[pm][:, 0:48], r=[FB.names[pm]], w=["modT"])
        FB.release(pm)
        TT("dve", modT[:], modT[:], badaT[:].unsqueeze(2).to_broadcast([128, 24, 2]), ALU.add, r=["modT", "colsT"], w=["modT"])
        CP("dve", shT[:], modT[:, 0:8, :], r=["modT"], w=["shT"])
        TS_("dve", g1T[:], modT[:, 8:16, :], 1.0, None, ALU.add, r=["modT"], w=["g1T"])
        TT("dve", g1T[:], g1T[:], normwT[:].unsqueeze(2).to_broadcast([128, 8, 2]), ALU.mult, r=["g1T", "colsT"], w=["g1T"])
        for c in range(2):
            for k in range(8):
                TS_("dve", dg[:], identf, modT[:, 16 + k, c:c + 1], None, ALU.mult, r=["cst", "modT"], w=["dg"])
                pg = FB.alloc()
                MM(fbank[pg][:, 0:128], lhsT=onesf, rhs=dg[:], r=["cst", "dg"], w=[FB.names[pg]])
                CP("act", gate_bc[:, c, k * 128:(k + 1) * 128], fbank[pg][:, 0:128], r=[FB.names[pg]], w=["gate_bc"])
                FB.release(pg)
        for k in range(8):
            st_, sk_ = nxt()
            DMA(st_[:, 0:D], w_out_d[l, k * 128:(k + 1) * 128, :], w=[sk_])
            if k % 2 == 0:
                TS_("dve", wout[:, k, :], st_[:, 0:D], rsT[:, k:k + 1], None, ALU.mult, r=[sk_, "rsT"], w=["wout"])
            else:
                ACT(wout[:, k, :], st_[:, 0:D], AF.Copy, r=[sk_, "rsT"], w=["wout"], scale=rsT[:, k:k + 1])

    def gen_H(sq, sc, prev_sc, has_next, descending):
        slot = sc % 2
        c = sq["cond"]
        kx = ("X", sq["id"], sc)
        for j in range(2):
            ch = sc * 2 + j
            DMA(xtH[j], sq["src"][ch * 128:(ch + 1) * 128, :], r=[kx], w=[f"xtH{j}"])
            ACT(xh[:], xtH[j], AF.Square, r=[f"xtH{j}"], w=["xh", "ss2"], accum_out=ss2[:, j:j + 1])
        rstd_ops(ss2[:], rs2[:], D, r=["ss2"], w=["rs2"])
        yield
        for j in range(2):
            TS_("dve", xh[:], xtH[j], rs2[:, j:j + 1], None, ALU.mult, r=[f"xtH{j}", "rs2"], w=["xh"])
            tb = yield from galloc(TB)
            for k in range(8):
                TR(tbank[tb][:, k * 128:(k + 1) * 128], xh[:, k * 128:(k + 1) * 128], identb, r=["xh", "cstb"], w=[TB.names[tb]])
            hv = hT[slot][:, :, 2 + j * 128:2 + (j + 1) * 128]
            TT("dve", hv, tbank[tb][:, :].rearrange("p (k t) -> p k t", k=8), g1T[:, :, c:c + 1].to_broadcast([128, 8, 128]), ALU.mult,
               r=[TB.names[tb], "g1T"], w=[("hT", slot, "m")])
            TB.release(tb)
            TT("dve", hv, hv, shT[:, :, c:c + 1].to_broadcast([128, 8, 128]), ALU.add, r=[("hT", slot, "m"), "shT"], w=[("hT", slot, "m")])
            yield
        lcols, rcols = slice(0, 2), slice(TS + 2, TS + 4)
        if prev_sc is None:
            first_side = "r" if descending else "l"
            MSET("pool", hT[slot][:, :, rcols if first_side == "r" else lcols], 0.0, w=[("hT", slot, first_side)])
        else:
            ps = prev_sc % 2
            if prev_sc > sc:
                CP("pool", hT[slot][:, :, rcols], hT[ps][:, :, 2:4], r=[("hT", ps, "m")], w=[("hT", slot, "r")])
                CP("pool", hT[ps][:, :, lcols], hT[slot][:, :, TS:TS + 2], r=[("hT", slot, "m")], w=[("hT", ps, "l")])
            else:
                CP("pool", hT[slot][:, :, lcols], hT[ps][:, :, TS:TS + 2], r=[("hT", ps, "m")], w=[("hT", slot, "l")])
                CP("pool", hT[ps][:, :, rcols], hT[slot][:, :, 2:4], r=[("hT", slot, "m")], w=[("hT", ps, "r")])
        if not has_next:
            last_side = "l" if descending else "r"
            MSET("pool", hT[slot][:, :, lcols if last_side == "l" else rcols], 0.0, w=[("hT", slot, last_side)])

    def hkeys(slot):
        return [("hT", slot, "m"), ("hT", slot, "l"), ("hT", slot, "r")]

    def gen_fm(slot, groups, with_qk, with_lr=True):
        def conv_part(g, xs_):
            pc_ = yield from galloc(FB)
            for t in range(5):
                MM(fbank[pc_][:, 0:TS], lhsT=cdiag[:, t, g, :], rhs=xpre[:, xs_, t:t + TS], start=(t == 0), stop=(t == 4),
                   r=["cdiag", ("xpre", xs_)], w=[FB.names[pc_]])
            ACT(xc[:, g, :], fbank[pc_][:, 0:TS], AF.Silu, r=[FB.names[pc_], "colsT"], w=[("xc", g)], bias=cbT[:, g:g + 1])
            FB.release(pc_)

        pend = None
        for gi, g in enumerate(groups):
            b = yield from galloc(FB)
            for k in range(8):
                MM(fbank[b][:, 0:TS + 4], lhsT=win[:, k, C_XBC + g * 128:C_XBC + (g + 1) * 128], rhs=hT[slot][:, k, :],
                   start=(k == 0), stop=(k == 7), r=["win"] + hkeys(slot), w=[FB.names[b]])
            xs_ = gi % 3
            CP("act", xpre[:, xs_, :], fbank[b][:, 0:TS + 4], r=[FB.names[b]], w=[("xpre", xs_)])
            FB.release(b)
            if pend is not None:
                yield from conv_part(*pend)
            pend = (g, xs_)
            yield
        if pend is not None:
            yield from conv_part(*pend)
            yield
        if with_qk:
            for which, c0 in ((0, C_GQ), (1, C_GK)):
                b = yield from galloc(FB)
                for k in range(8):
                    MM(fbank[b][:, 0:TS], lhsT=win[:, k, c0:c0 + 128], rhs=hT[slot][:, k, 2:2 + TS],
                       start=(k == 0), stop=(k == 7), r=["win", ("hT", slot, "m")], w=[FB.names[b]])
                CP("act", gqk[:, which, :], fbank[b][:, 0:TS], r=[FB.names[b]], w=["gqk"])
                FB.release(b)
                yield
        if not with_lr:
            return
        b = yield from galloc(FB)
        for k in range(8):
            MM(fbank[b][0:32, 0:TS], lhsT=win[:, k, C_LR:C_LR + 32], rhs=hT[slot][:, k, 2:2 + TS],
               start=(k == 0), stop=(k == 7), r=["win", ("hT", slot, "m")], w=[FB.names[b]])
        CP("act", lrT[0:32, :], fbank[b][0:32, 0:TS], r=[FB.names[b]], w=["lrT"])
        FB.release(b)

    def tm_mm(b, slot, j, c0, n):
        for k in range(8):
            MM(fbank[b][:, 0:n], lhsT=hT[slot][:, k, 2 + j * 128:2 + (j + 1) * 128], rhs=win[:, k, c0:c0 + n],
               start=(k == 0), stop=(k == 7), r=["win", ("hT", slot, "m")], w=[FB.names[b]])

    def gen_tm(slot, j, p, full, ch=None):
        b = yield from galloc(FB)
        tm_mm(b, slot, j, C_AQ if full else C_KV, 256)
        CP("act", raw[p][:], fbank[b][:, 0:256], r=[FB.names[b]], w=[f"raw{p}"])
        FB.release(b)
        yield
        if not full:
            b = yield from galloc(FB)
            tm_mm(b, slot, j, C_Z, 512)
            ACT(sz[p][:], fbank[b][:, :], AF.Silu, r=[FB.names[b]], w=[f"sz{p}"])
            FB.release(b)
            DMA(sz_scr[ch], sz[p][:], r=[f"sz{p}"], w=[("szs", ch)])
            yield
            b = yield from galloc(FB)
            tm_mm(b, slot, j, C_B3, 512)
            ACT(sg[p][:], fbank[b][:, :], AF.Silu, r=[FB.names[b]], w=[f"sg{p}"])
            FB.release(b)
            DMA(sg_scr[ch], sg[p][:], r=[f"sg{p}"], w=[("sgs", ch)])
            yield
        if full:
            DMA(sz[p][:], sz_scr[ch], r=[("szs", ch)], w=[f"sz{p}"])
            DMA(sg[p][:], sg_scr[ch], r=[("sgs", ch)], w=[f"sg{p}"])
            DMA(dtx[p][:], dx_scr[ch], r=[("dxs", ch)], w=[f"dtx{p}"])
            DMA(ktok[p][:], kt_scr[ch], r=[("kts", ch)], w=[f"ktok{p}"])
            DMA(vtok[p][:], vt_scr[ch], r=[("vts", ch)], w=[f"vtok{p}"])
            return
        b = yield from galloc(FB)
        tm_mm(b, slot, j, C_B2, 400)
        TT("dve", dtx[p][:], fbank[b][:, 0:16], dtb_bc[:], ALU.add, r=[FB.names[b], "dtb_bc"], w=[f"dtx{p}"])
        CP("act", ktok[p][:], fbank[b][:, 16:144], r=[FB.names[b]], w=[f"ktok{p}"])
        CP("act", vtok[p][:], fbank[b][:, 144:400], r=[FB.names[b]], w=[f"vtok{p}"])
        FB.release(b)
        DMA(dx_scr[ch], dtx[p][:], r=[f"dtx{p}"], w=[("dxs", ch)])
        DMA(kt_scr[ch], ktok[p][:], r=[f"ktok{p}"], w=[("kts", ch)])
        DMA(vt_scr[ch], vtok[p][:], r=[f"vtok{p}"], w=[("vts", ch)])

    def softplus_dt(p):
        ACT(dte[:], dtx[p][:], AF.Exp, r=[f"dtx{p}"], w=["dte"])
        ACT(dtt[:], dte[:], AF.Ln, r=["dte"], w=["dtt"], bias=1.0)
        TT("dve", aa[:], dtt[:], A_bc[:], ALU.mult, r=["dtt", "A_bc"], w=["aa"])

    def gates_sp(b, j, cols):
        MM(fbank[b][:, 0:256], lhsT=lrT[0:33, j * 128:(j + 1) * 128], rhs=gkup[0:33, :], r=["lrT", "lrT1", "gkup"], w=[FB.names[b]])
        ACT(sp_[:, cols], fbank[b][:, cols], AF.Exp, r=[FB.names[b]], w=["sp_"], scale=-1.0)
        FB.release(b)
        TS_("dve", sp_[:, cols], sp_[:, cols], 1e30, None, ALU.min, r=["sp_"], w=["sp_"])
        ACT(sp_[:, cols], sp_[:, cols], AF.Ln, r=["sp_"], w=["sp_"], bias=1.0)

    def xs_tokmajor(tb, j):
        for g in range(4):
            TR(tbank[tb][:, g * 128:(g + 1) * 128], xc[:, g, j * 128:(j + 1) * 128], identb, r=[("xc", g), "cstb"], w=[TB.names[tb]])

    def b_tokmajor(tb, j):
        for g in range(2):
            TR(tbank[tb][:, g * 128:(g + 1) * 128], xc[:, 4 + g, j * 128:(j + 1) * 128], identb, r=[("xc", 4 + g), "cstb"], w=[TB.names[tb]])
        CP("act", Btok[:], tbank[tb][:, 0:256], r=[TB.names[tb]], w=["Btok"])
        TB.release(tb)

    def bc8(ap):
        return ap.unsqueeze(2).to_broadcast([128, 8, 64])

    def v3(ap, h=8):
        return ap.rearrange("p (h e) -> p h e", h=h)

    def rope_apply(src, dst, H, rt, rk, r_src, w_dst):
        x5 = src.rearrange("p (h a b e) -> p h a b e", h=H, a=2, b=2)
        cosb = rt[:, 0:64].unsqueeze(1).to_broadcast([128, H, 64])
        s4 = rt[:, 64:128].rearrange("p (a b e) -> p a b e", a=2, b=2)
        TT("dve", v3(qr1[:, 0:H * 64], H), v3(src, H), cosb, ALU.mult, r=r_src + [rk], w=["qr1"])
        q5 = qr2[:, 0:H * 64].rearrange("p (h a b e) -> p h a b e", h=H, a=2, b=2)
        TT("dve", q5[:, :, :, 0, :], x5[:, :, :, 1, :], s4[:, :, 0, :].unsqueeze(1).to_broadcast([128, H, 2, 16]), ALU.mult,
           r=r_src + [rk], w=["qr2"])
        TT("dve", q5[:, :, :, 1, :], x5[:, :, :, 0, :], s4[:, :, 1, :].unsqueeze(1).to_broadcast([128, H, 2, 16]), ALU.mult,
           r=r_src + [rk], w=["qr2"])
        TT("dve", dst, qr1[:, 0:H * 64], qr2[:, 0:H * 64], ALU.add, r=["qr1", "qr2"], w=w_dst)

    def gen_ssdA(sq, sc, j):
        ch = sc * 2 + j
        p = ch % 2
        softplus_dt(p)
        pw = yield from galloc(FB)
        MM(fbank[pw][:, 0:8], lhsT=mf(K_LT), rhs=aa[:, 8:16], r=["cst", "aa"], w=[FB.names[pw]])
        MM(fbank[pw][:, 8:16], lhsT=onesf, rhs=aa[:, 8:16], r=["cst", "aa"], w=[FB.names[pw]])
        ACT(ecs[:, 0:16], fbank[pw][:, 0:16], AF.Exp, r=[FB.names[pw]], w=["ecs"])
        FB.release(pw)
        TT("dve", mdt[:], dtt[:, 8:16], ecs[:, 0:8], ALU.mult, r=["dtt", "ecs"], w=["mdt"])
        yield
        tx = yield from galloc(TB)
        xs_tokmajor(tx, j)
        TT("dve", v3(xdec[:]), v3(tbank[tx][:, 0:512]), bc8(mdt[:]), ALU.mult, r=[TB.names[tx], "mdt"], w=["xdec"])
        TB.release(tx)
        yield
        tb = yield from galloc(TB)
        b_tokmajor(tb, j)
        DMA(bt_scr[ch], Btok[:], r=["Btok"], w=[("bts", ch)])
        yield
        pst = yield from galloc(FB)
        for g in range(2):
            MM(fbank[pst][:, g * 256:(g + 1) * 256], lhsT=Btok[:, g * 128:(g + 1) * 128], rhs=xdec[:, g * 256:(g + 1) * 256],
               r=["Btok", "xdec"], w=[FB.names[pst]])
        CP("dve", Hio[:], Hb[:], r=["Hb"], w=["Hio"])
        DMA(hb_scr[ch], Hio[:], r=["Hio"], w=[("hbs", ch)])
        TT("dve", v3(Hb[:]), v3(Hb[:]), bc8(ecs[:, 8:16]), ALU.mult, r=["Hb", "ecs"], w=["Hb"])
        TT("dve", Hb[:], Hb[:], fbank[pst][:, :], ALU.add, r=["Hb", FB.names[pst]], w=["Hb"])
        FB.release(pst)

    def gen_glaA(sq, sc, j):
        ch = sc * 2 + j
        p = ch % 2
        b = yield from galloc(FB)
        gates_sp(b, j, slice(128, 256))
        yield
        pg = yield from galloc(FB)
        MM(fbank[pg][:, 0:128], lhsT=mf(K_LT), rhs=sp_[:, 128:256], r=["cst", "sp_"], w=[FB.names[pg]])
        MM(fbank[pg][:, 128:129], lhsT=sp_[:, 128:256], rhs=cst[:, K_ONE:K_ONE + 1], r=["cst", "sp_"], w=[FB.names[pg]])
        ACT(ekd[:], fbank[pg][:, 0:128], AF.Exp, r=[FB.names[pg]], w=["ekd"], scale=-1.0 / 16)
        ACT(egt[:, 1:2], fbank[pg][:, 128:129], AF.Exp, r=[FB.names[pg]], w=["egt"], scale=-1.0 / 16)
        FB.release(pg)
        TT("dve", kdec[:], ktok[p][:], ekd[:], ALU.mult, r=[f"ktok{p}", "ekd"], w=["kdec"])
        yield
        pS = yield from galloc(FB)
        MM(fbank[pS][:, 0:256], lhsT=kdec[:], rhs=vtok[p][:], r=["kdec", f"vtok{p}"], w=[FB.names[pS]])
        CP("dve", Sio[:], Sb[:], r=["Sb"], w=["Sio"])
        DMA(sb_scr[ch], Sio[:], r=["Sio"], w=[("sbs", ch)])
        TT("dve", stmp[:], fbank[pS][:, 0:256], cst[:, KF_BLKS:KF_BLKS + 256], ALU.mult, r=[FB.names[pS], "cst"], w=["stmp"])
        FB.release(pS)
        STT("dve", Sb[:], Sb[:], egt[:, 1:2], stmp[:], ALU.mult, ALU.add, r=["Sb", "egt", "stmp"], w=["Sb"])

    def gen_kvA(sq, sc, j, l):
        ch = sc * 2 + j
        slot = sc % 2
        sample = sq["kind"] == "s"
        if sample:
            DMA(ropet[j][:], rope_d[ch * 128:(ch + 1) * 128, :], w=[f"ropet{j}"])
        p = ch % 2
        rw, rk_ = raw[p], f"raw{p}"
        for h in range(2):
            ACT(sj[:, 256 + h * 64:256 + (h + 1) * 64], rw[:, h * 64:(h + 1) * 64], AF.Square, r=[rk_], w=["sja", "ssq"],
                accum_out=ssq[:, h:h + 1])
        rstd_ops(ssq[:, 0:2], rsq[:, 0:2], 64, r=["ssq"], w=["rsq"])
        for h in range(2):
            STT("dve", qn[:, h * 64:(h + 1) * 64], rw[:, h * 64:(h + 1) * 64], rsq[:, h:h + 1], qkw_bc[:, 4 + h, :],
                ALU.mult, ALU.mult, r=[rk_, "rsq", "qkw_bc"], w=["qn"])
        CP("pool", Vt[:, ch, :, 0:64], rw[:, 128:256].rearrange("p (j e) -> p j e", j=2), r=[rk_], w=[("Vt", ch)])
        if not sample:
            DMA(nk_d[sq["pi"], l, j * 128:(j + 1) * 128, :], qn[:, 0:128], r=["qn"], w=[("nk", sq["pi"], l, j)])
            DMA(nv_d[sq["pi"], l, j * 128:(j + 1) * 128, :], rw[:, 128:256], r=[rk_], w=[("nv", sq["pi"], l, j)])
        yield
        if sample:
            rope_apply(qn[:, 0:128], knb[:], 2, ropet[j], f"ropet{j}", ["qn"], ["knb"])
        else:
            CP("dve", knb[:], qn[:, 0:128], r=["qn"], w=["knb"])
        yield
        tb = yield from galloc(TB)
        TR(tbank[tb][:, 0:128], knb[:], identb, r=["knb", "cstb"], w=[TB.names[tb]])
        CP("act", KT[:, ch * 128:(ch + 1) * 128], tbank[tb][:, 0:128], r=[TB.names[tb]], w=[("KT", ch)])
        TB.release(tb)

    def gen_ssdB(sq, sc, j):
        ch = sc * 2 + j
        p = ch % 2
        js = slice(j * 128, (j + 1) * 128)
        AA, DT = aa, dtt
        ka, kd = "aa", "dtt"
        DMA(Hio[:], hb_scr[ch], r=[("hbs", ch)], w=["Hio"])
        softplus_dt(p)
        pc = yield from galloc(FB)
        MM(fbank[pc][:, 0:8], lhsT=mf(K_LE), rhs=AA[:, 0:8], r=["cst", ka], w=[FB.names[pc]])
        MM(fbank[pc][:, 8:16], lhsT=mf(K_GE), rhs=AA[:, 8:16], r=["cst", ka], w=[FB.names[pc]])
        MM(fbank[pc][:, 16:24], lhsT=mf(K_GT), rhs=AA[:, 0:8], r=["cst", ka], w=[FB.names[pc]])
        MM(fbank[pc][:, 24:32], lhsT=onesf, rhs=AA[:, 0:8], r=["cst", ka], w=[FB.names[pc]])
        ACT(ecs[:, 0:32], fbank[pc][:, 0:32], AF.Exp, r=[FB.names[pc]], w=["ecs"])
        FB.release(pc)
        for d_, mk in ((0, K_GT), (1, K_LT)):
            TT("dve", Lh[:, d_ * 8:(d_ + 1) * 8, :], mb(mk).unsqueeze(1).to_broadcast([128, 8, 128]),
               AA[:, d_ * 8:(d_ + 1) * 8].unsqueeze(2).to_broadcast([128, 8, 128]), ALU.mult, r=["cstb", ka], w=[("Lh", d_)])
        yield
        pG = yield from galloc(FB)
        for g in range(2):
            MM(fbank[pG][:, g * 128:(g + 1) * 128], lhsT=xc[:, 4 + g, js], rhs=xc[:, 6 + g, js], r=[("xc", 4 + g), ("xc", 6 + g)], w=[FB.names[pG]])
        for d_, mk in ((0, K_LE), (1, K_GE)):
            TT("dve", GM[:, d_, :, :], fbank[pG][:, 0:256].rearrange("p (g q) -> p g q", g=2), mf(mk).unsqueeze(1).to_broadcast([128, 2, 128]),
               ALU.mult, r=[FB.names[pG], "cst"], w=["GM"])
        FB.release(pG)
        yield
        for d_, mk in ((0, K_LE), (1, K_GE)):
            p0 = yield from galloc(FB)
            p1 = yield from galloc(FB)
            for h in range(8):
                pb = p0 if h < 4 else p1
                MM(fbank[pb][:, (h % 4) * 128:(h % 4 + 1) * 128], lhsT=Lh[:, d_ * 8 + h, :], rhs=mb(mk), r=[("Lh", d_), "cstb"], w=[FB.names[pb]])
            ACT(eseg[:, d_ * 8:d_ * 8 + 4, :], fbank[p0][:, :].rearrange("p (h q) -> p h q", h=4), AF.Exp, r=[FB.names[p0]], w=[("eseg", d_, 0)])
            ACT(eseg[:, d_ * 8 + 4:d_ * 8 + 8, :], fbank[p1][:, :].rearrange("p (h q) -> p h q", h=4), AF.Exp, r=[FB.names[p1]], w=[("eseg", d_, 1)])
            FB.release(p0)
            FB.release(p1)
            for g in range(2):
                TT("dve", eseg[:, d_ * 8 + g * 4:d_ * 8 + g * 4 + 4, :], eseg[:, d_ * 8 + g * 4:d_ * 8 + g * 4 + 4, :],
                   GM[:, d_, g:g + 1, :].to_broadcast([128, 4, 128]), ALU.mult, r=[("eseg", d_, g), "GM"], w=[("eseg", d_, g)])
            yield
        tx = yield from galloc(TB)
        xs_tokmajor(tx, j)
        px = v3(tbank[tx][:, 0:512])
        TT("dve", v3(xdt[:, 0, :]), px, bc8(DT[:, 0:8]), ALU.mult, r=[TB.names[tx], kd], w=["xdt"])
        TT("dve", v3(xdt[:, 1, :]), px, bc8(DT[:, 8:16]), ALU.mult, r=[TB.names[tx], kd], w=["xdt"])
        TT("dve", mdt[:], DT[:, 0:8], ecs[:, 16:24], ALU.mult, r=[kd, "ecs"], w=["mdt"])
        TT("dve", v3(xdec[:]), px, bc8(mdt[:]), ALU.mult, r=[TB.names[tx], "mdt"], w=["xdec"])
        TB.release(tx)
        yield
        DMA(Btok[:], bt_scr[ch], r=[("bts", ch)], w=["Btok"])
        pY = yield from galloc(FB)
        for g in range(4):
            MM(fbank[pY][:, g * 128:(g + 1) * 128], lhsT=xc[:, g, js], rhs=diagD[:, g, :], start=(g == 0), stop=False,
               r=[("xc", g), "diagD"], w=[FB.names[pY]])
        for h in range(8):
            MM(fbank[pY][:, h * 64:(h + 1) * 64], lhsT=eseg[:, h, :], rhs=xdt[:, 0, h * 64:(h + 1) * 64], start=False, stop=False,
               r=[("eseg", 0, h // 4), "xdt"], w=[FB.names[pY]])
            MM(fbank[pY][:, h * 64:(h + 1) * 64], lhsT=eseg[:, 8 + h, :], rhs=xdt[:, 1, h * 64:(h + 1) * 64], start=False, stop=(h == 7),
               r=[("eseg", 1, h // 4), "xdt"], w=[FB.names[pY]])
        yield
        pOf = yield from galloc(FB)
        pOb = yield from galloc(FB)
        for g in range(2):
            MM(fbank[pOf][:, g * 256:(g + 1) * 256], lhsT=xc[:, 6 + g, js], rhs=Hfb[:, g * 256:(g + 1) * 256], r=[("xc", 6 + g), "Hfb"], w=[FB.names[pOf]])
            MM(fbank[pOb][:, g * 256:(g + 1) * 256], lhsT=xc[:, 6 + g, js], rhs=Hio[:, g * 256:(g + 1) * 256], r=[("xc", 6 + g), "Hio"], w=[FB.names[pOb]])
        TT("dve", v3(ysb[:]), v3(fbank[pOf][:, :]), bc8(ecs[:, 0:8]), ALU.mult, r=[FB.names[pOf], "ecs"], w=["ysb"])
        TT("dve", v3(yt2[:]), v3(fbank[pOb][:, :]), bc8(ecs[:, 8:16]), ALU.mult, r=[FB.names[pOb], "ecs"], w=["yt2"])
        FB.release(pOf)
        FB.release(pOb)
        TT("dve", ysb[:], ysb[:], yt2[:], ALU.add, r=["ysb", "yt2"], w=["ysb"])
        TT("dve", ysb[:], ysb[:], fbank[pY][:, :], ALU.add, r=["ysb", FB.names[pY]], w=["ysb"])
        FB.release(pY)
        yield
        pst = yield from galloc(FB)
        for g in range(2):
            MM(fbank[pst][:, g * 256:(g + 1) * 256], lhsT=Btok[:, g * 128:(g + 1) * 128], rhs=xdec[:, g * 256:(g + 1) * 256],
               r=["Btok", "xdec"], w=[FB.names[pst]])
        TT("dve", v3(Hf[:]), v3(Hf[:]), bc8(ecs[:, 24:32]), ALU.mult, r=["Hf", "ecs"], w=["Hf"])
        TT("dve", Hf[:], Hf[:], fbank[pst][:, :], ALU.add, r=["Hf", FB.names[pst]], w=["Hf"])
        FB.release(pst)
        CP("pool", Hfb[:], Hf[:], r=["Hf"], w=["Hfb"])
        yield
        TT("dve", ysb[:], ysb[:], sz[p][:], ALU.mult, r=["ysb", f"sz{p}"], w=["ysb"])
        ACT(xdec[:], ysb[:], AF.Square, r=["ysb"], w=["xdec", "ssy"], accum_out=ssy[:, 0:1])
        rstd_ops(ssy[:], rsy[:], 512, r=["ssy"], w=["rsy"])
        ACT(ymix[p][:, 0:512], ysb[:], AF.Copy, r=["ysb", "rsy"], w=[("ymix", p, 0)], scale=rsy[:, 0:1])

    def gen_glaB(sq, sc, j):
        ch = sc * 2 + j
        p = ch % 2
        js = slice(j * 128, (j + 1) * 128)
        DMA(Sio[:], sb_scr[ch], r=[("sbs", ch)], w=["Sio"])
        b = yield from galloc(FB)
        gates_sp(b, j, slice(0, 256))
        yield
        pgc = yield from galloc(FB)
        MM(fbank[pgc][:, 0:128], lhsT=sp_[:, 0:128], rhs=mf(K_LE), r=["sp_", "cst"], w=[FB.names[pgc]])
        MM(fbank[pgc][:, 128:256], lhsT=sp_[:, 128:256], rhs=mf(K_GE), r=["sp_", "cst"], w=[FB.names[pgc]])
        MM(fbank[pgc][:, 256:384], lhsT=mf(K_GT), rhs=sp_[:, 0:128], r=["sp_", "cst"], w=[FB.names[pgc]])
        ACT(egq[:].rearrange("p d t -> p (d t)"), fbank[pgc][:, 0:256], AF.Exp, r=[FB.names[pgc]], w=["egq"], scale=-1.0 / 16)
        ACT(egk[:].rearrange("p d t -> p (d t)"), fbank[pgc][:, 0:256], AF.Exp, r=[FB.names[pgc]], w=["egk"], scale=1.0 / 16)
        ACT(ekd[:], fbank[pgc][:, 256:384], AF.Exp, r=[FB.names[pgc]], w=["ekd"], scale=-1.0 / 16)
        ACT(egt[:, 0:1], fbank[pgc][:, 127:128], AF.Exp, r=[FB.names[pgc]], w=["egt"], scale=-1.0 / 16)
        FB.release(pgc)
        yield
        for d_ in range(2):
            STT("dve", qin[:, d_, :], gqk[:, 0, js], 32 ** -0.5, egq[:, d_, :], ALU.mult, ALU.mult, r=["gqk", "egq"], w=["qin"])
            TT("dve", kin[:, d_, :], gqk[:, 1, js], egk[:, d_, :], ALU.mult, r=["gqk", "egk"], w=["kin"])
            TT("dve", Qblk[:, d_, :].rearrange("p (h q) -> p h q", h=4), qin[:, d_:d_ + 1, :].to_broadcast([128, 4, 128]),
               cstb[:, KB_BLK4:KB_BLK4 + 512].rearrange("p (h q) -> p h q", h=4), ALU.mult, r=["qin", "cstb"], w=["Qblk"])
        yield
        pOG = yield from galloc(FB)
        MM(fbank[pOG][:, 0:256], lhsT=qin[:, 0, :], rhs=Sfb[:], start=True, stop=False, r=["qin", "Sfb"], w=[FB.names[pOG]])
        MM(fbank[pOG][:, 0:256], lhsT=qin[:, 1, :], rhs=Sio[:], start=False, stop=False, r=["qin", "Sio"], w=[FB.names[pOG]])
        for d_, mk in ((0, K_LE), (1, K_GE)):
            pAT = yield from galloc(FB)
            MM(fbank[pAT][:, :], lhsT=kin[:, d_, :], rhs=Qblk[:, d_, :], r=["kin", "Qblk"], w=[FB.names[pAT]])
            TT("dve", attm[:, d_, :].rearrange("p (h q) -> p h q", h=4), fbank[pAT][:, :].rearrange("p (h q) -> p h q", h=4),
               mf(mk).unsqueeze(1).to_broadcast([128, 4, 128]), ALU.mult, r=[FB.names[pAT], "cst"], w=["attm"])
            FB.release(pAT)
            for h in range(4):
                MM(fbank[pOG][:, h * 64:(h + 1) * 64], lhsT=attm[:, d_, h * 128:(h + 1) * 128], rhs=vtok[p][:, h * 64:(h + 1) * 64],
                   start=False, stop=(d_ == 1 and h == 3), r=["attm", f"vtok{p}"], w=[FB.names[pOG]])
            yield
        TT("dve", kdec[:], ktok[p][:], ekd[:], ALU.mult, r=[f"ktok{p}", "ekd"], w=["kdec"])
        pS = yield from galloc(FB)
        MM(fbank[pS][:, 0:256], lhsT=kdec[:], rhs=vtok[p][:], r=["kdec", f"vtok{p}"], w=[FB.names[pS]])
        TT("dve", stmp[:], fbank[pS][:, 0:256], cst[:, KF_BLKS:KF_BLKS + 256], ALU.mult, r=[FB.names[pS], "cst"], w=["stmp"])
        FB.release(pS)
        STT("dve", Sf[:], Sf[:], egt[:, 0:1], stmp[:], ALU.mult, ALU.add, r=["Sf", "egt", "stmp"], w=["Sf"])
        CP("pool", Sfb[:], Sf[:], r=["Sf"], w=["Sfb"])
        yield
        for h in range(4):
            ACT(sj[:, h * 64:(h + 1) * 64], fbank[pOG][:, h * 64:(h + 1) * 64], AF.Square, r=[FB.names[pOG]], w=["sjg", "sso"],
                accum_out=sso[:, h:h + 1])
        rstd_ops(sso[:], rso[:], 64, r=["sso"], w=["rso"])
        TT("dve", v3(og[:], 4), v3(fbank[pOG][:, 0:256], 4), rso[:].unsqueeze(2).to_broadcast([128, 4, 64]), ALU.mult,
           r=[FB.names[pOG], "rso"], w=["og"])
        FB.release(pOG)
        TT("pool", ymix[p][:, 512:768], og[:], sg[p][:, 0:256], ALU.mult, r=["og", f"sg{p}"], w=[("ymix", p, 1)])

    def gen_attB(sq, sc, j):
        ch = sc * 2 + j
        p = ch % 2
        slot = sc % 2
        sample = sq["kind"] == "s"
        nch = sq["L"] // 128
        if sample:
            DMA(ropet[j][:], rope_d[ch * 128:(ch + 1) * 128, :], w=[f"ropet{j}"])
        rw, rk_ = raw[p], f"raw{p}"
        for h in range(4):
            ACT(sj[:, 256 + h * 64:256 + (h + 1) * 64], rw[:, h * 64:(h + 1) * 64], AF.Square, r=[rk_], w=["sja", "ssq"],
                accum_out=ssq[:, h:h + 1])
        rstd_ops(ssq[:], rsq[:], 64, r=["ssq"], w=["rsq"])
        TT("dve", v3(qn[:], 4), v3(rw[:], 4), rsq[:].unsqueeze(2).to_broadcast([128, 4, 64]), ALU.mult,
           r=[rk_, "rsq"], w=["qn"])
        TT("dve", v3(qn[:], 4), v3(qn[:], 4), qkw_bc[:, 0:4, :], ALU.mult, r=["qn", "qkw_bc"], w=["qn"])
        yield
        if sample:
            rope_apply(qn[:], qr1[:], 4, ropet[j], f"ropet{j}", ["qn"], ["qr1"])
            qsrc, qk_ = qr1, "qr1"
        else:
            qsrc, qk_ = qn, "qn"
        CP("act", qb[:].rearrange("p (g j e) -> p j g e", g=2, j=2), qsrc[:].rearrange("p (j g e) -> p j g e", j=2, g=2), r=[qk_], w=["qb"])
        yield
        tb = yield from galloc(TB)
        for g in range(2):
            TR(tbank[tb][:, g * 128:(g + 1) * 128], qb[:, g * 128:(g + 1) * 128], identb, r=["qb", "cstb"], w=[TB.names[tb]])
        CP("act", qT[:].rearrange("p g t -> p (g t)"), tbank[tb][:, 0:256], r=[TB.names[tb]], w=["qT"])
        TB.release(tb)
        yield
        blocks = []
        if sample:
            for bb in range(2):
                blocks.append((KTc[:, bb * 128:(bb + 1) * 128], Vc[:, bb, :, :], None, ["KTc"], ["Vc", "Vc1"]))
            for cc, mk in ((ch - 1, K_GE), (ch, None), (ch + 1, K_LE)):
                if 0 <= cc < nch:
                    blocks.append((KT[:, cc * 128:(cc + 1) * 128], Vt[:, cc, :, :], mk, [("KT", cc)], [("Vt", cc), "Vt1"]))
        else:
            for cc in range(nch):
                blocks.append((KT[:, cc * 128:(cc + 1) * 128], Vt[:, cc, :, :], None, [("KT", cc)], [("Vt", cc), "Vt1"]))
        pAV = yield from galloc(FB)
        for jh in range(2):
            for bi_, (kap, vap, mk, kdeps, vdeps) in enumerate(blocks):
                pSc = yield from galloc(FB)
                MM(fbank[pSc][:, 0:256], lhsT=kap[jh * 64:(jh + 1) * 64, :], rhs=qT[jh * 64:(jh + 1) * 64, :, :].rearrange("p g t -> p (g t)"),
                   r=kdeps + ["qT"], w=[FB.names[pSc]])
                ACT(PT[:, bi_, :], fbank[pSc][:, 0:256], AF.Exp, r=[FB.names[pSc]], w=[("PT", bi_)], scale=0.125)
                FB.release(pSc)
                if mk is not None:
                    TT("pool", PT[:, bi_, :].rearrange("p (g q) -> p g q", g=2), PT[:, bi_, :].rearrange("p (g q) -> p g q", g=2),
                       mb(mk).unsqueeze(1).to_broadcast([128, 2, 128]), ALU.mult, r=[("PT", bi_), "cstb"], w=[("PT", bi_)])
            yield
            for g in range(2):
                hh = jh * 2 + g
                for bi_, (kap, vap, mk, kdeps, vdeps) in enumerate(blocks):
                    MM(fbank[pAV][:, hh * 65:(hh + 1) * 65], lhsT=PT[:, bi_, g * 128:(g + 1) * 128], rhs=vap[:, jh, :],
                       start=(bi_ == 0), stop=(bi_ == len(blocks) - 1), r=[("PT", bi_)] + vdeps, w=[FB.names[pAV]])
            yield
        av = fbank[pAV][:, 0:260].rearrange("p (h e) -> p h e", h=4)
        TT("dve", den[:].unsqueeze(2), av[:, :, 64:65], esink[:].unsqueeze(2), ALU.add, r=[FB.names[pAV], "esink"], w=["den"])
        S.add("dve", lambda e: e.reciprocal(out=den[:], in_=den[:]), ["den"], ["den"])
        TT("dve", v3(oat[:], 4), av[:, :, 0:64], den[:].unsqueeze(2).to_broadcast([128, 4, 64]), ALU.mult, r=[FB.names[pAV], "den"], w=["oat"])
        FB.release(pAV)
        TT("pool", ymix[p][:, 768:1024], oat[:], sg[p][:, 256:512], ALU.mult, r=["oat", f"sg{p}"], w=[("ymix", p, 2)])

    def gen_out(sq, sc, j):
        ch = sc * 2 + j
        p = ch % 2
        c = sq["cond"]
        kx = ("X", sq["id"], sc)
        DMA(xtO[p][:], sq["src"][ch * 128:(ch + 1) * 128, :], r=[kx], w=[f"xtO{p}"])
        tb = yield from galloc(TB)
        for k in range(8):
            TR(tbank[tb][:, k * 128:(k + 1) * 128], ymix[p][:, k * 128:(k + 1) * 128], identb,
               r=[("ymix", p, 0), ("ymix", p, 1), ("ymix", p, 2), "cstb"], w=[TB.names[tb]])
        CP("act", ymT[:].rearrange("p k t -> p (k t)"), tbank[tb][:, :], r=[TB.names[tb]], w=["ymT"])
        TB.release(tb)
        yield
        for nb in range(2):
            po = yield from galloc(FB)
            for k in range(8):
                MM(fbank[po][:, :], lhsT=ymT[:, k, :], rhs=wout[:, k, nb * 512:(nb + 1) * 512], start=(k == 0), stop=(k == 7),
                   r=["ymT", "wout"], w=[FB.names[po]])
            TT("dve", xo[:], fbank[po][:, :], gate_bc[:, c, nb * 512:(nb + 1) * 512], ALU.mult,
               r=[FB.names[po], "gate_bc"], w=["xo"])
            FB.release(po)
            TT("pool", xtO[p][:, nb * 512:(nb + 1) * 512], xtO[p][:, nb * 512:(nb + 1) * 512], xo[:], ALU.add,
               r=["xo", f"xtO{p}"], w=[f"xtO{p}"])
            yield
        DMA(sq["dst"][ch * 128:(ch + 1) * 128, :], xtO[p][:], r=[f"xtO{p}"], w=[kx])

    def seq_setup(sq, l):
        if sq["kind"] == "s":
            DMA(ckst[:], ck_d[l].rearrange("(b p) f -> p b f", p=128), w=["og"])
            CP("dve", ckb[:], ckst[:], r=["og"], w=["ckb"])
            tb = TB.alloc()
            for bb in range(2):
                TR(tbank[tb][:, bb * 128:(bb + 1) * 128], ckb[:, bb, :], identb, r=["ckb", "cstb"], w=[TB.names[tb]])
            CP("act", KTc[:], tbank[tb][:, 0:256], r=[TB.names[tb]], w=["KTc"])
            TB.release(tb)
            DMA(ckst[:], cv_d[l].rearrange("(b p) f -> p b f", p=128), w=["og"])
            CP("dve", Vc[:, :, :, 0:64], ckst[:].rearrange("p b (j e) -> p b j e", j=2), r=["og"], w=["Vc"])
            for d_, Ht in ((0, Hf), (1, Hb)):
                DMA(sti[:], sssd_d[l, d_].rearrange("(b p) n -> p b n", p=128), w=["ysb"])
                pb = FB.alloc()
                for bb in range(4):
                    TR(fbank[pb][:, bb * 128:(bb + 1) * 128], sti[:, bb, :], identf, r=["ysb", "cst"], w=[FB.names[pb]])
                CP("dve", Ht[:], fbank[pb][:, :], r=[FB.names[pb]], w=["Hf" if d_ == 0 else "Hb"])
                FB.release(pb)
            for d_, St in ((0, Sf), (1, Sb)):
                kname = "Sf" if d_ == 0 else "Sb"
                MSET("dve", St[:], 0.0, w=[kname])
                for h in range(4):
                    DMA(St[h * 32:(h + 1) * 32, h * 64:(h + 1) * 64], sgla_d[l, d_, h * 32:(h + 1) * 32, :], w=[kname])
        else:
            for t_, kn in ((Hf, "Hf"), (Hb, "Hb"), (Sf, "Sf"), (Sb, "Sb")):
                MSET("dve", t_[:], 0.0, w=[kn])
        CP("act", Hfb[:], Hf[:], r=["Hf"], w=["Hfb"])
        CP("act", Sfb[:], Sf[:], r=["Sf"], w=["Sfb"])

    def write_states(sq, l, d_):
        Ht, hk = (Hf, "Hf") if d_ == 0 else (Hb, "Hb")
        St, sk = (Sf, "Sf") if d_ == 0 else (Sb, "Sb")
        pb = FB.alloc()
        for bb in range(4):
            TR(fbank[pb][:, bb * 128:(bb + 1) * 128], Ht[:, bb * 128:(bb + 1) * 128], identf, r=[hk, "cst"], w=[FB.names[pb]])
        CP("dve", sti[:].rearrange("p b n -> p (b n)"), fbank[pb][:, :], r=[FB.names[pb]], w=["ysb"])
        FB.release(pb)
        DMA(nssd_d[sq["pi"], l, d_].rearrange("(b p) n -> p b n", p=128), sti[:], r=["ysb"], w=[("nssd", sq["pi"], l, d_)])
        for h in range(4):
            DMA(ngla_d[sq["pi"], l, d_, h * 32:(h + 1) * 32, :], St[h * 32:(h + 1) * 32, h * 64:(h + 1) * 64], r=[sk],
                w=[("ngla", sq["pi"], l, d_, h)])

    seqs = [dict(kind="s", id="s", L=LS, cond=0, nsc=LS // TS)]
    for pi in range(NPR):
        seqs.append(dict(kind="p", id=f"p{pi}", pi=pi, L=LP, cond=1, nsc=1))
    for l in range(NL):
        load_layer(l)
        for sq in seqs:
            if sq["kind"] == "s":
                sq["src"] = xs_d if l == 0 else ys_d
                sq["dst"] = ys_d
            else:
                pi = sq["pi"]
                sq["src"] = (xp_d if l == 0 else yp_d)[pi * LP:(pi + 1) * LP, :]
                sq["dst"] = yp_d[pi * LP:(pi + 1) * LP, :]
            nsc = sq["nsc"]
            seq_setup(sq, l)
            order = list(range(nsc - 1, -1, -1))

            def mkH(i, desc):
                if i >= len(order):
                    return None
                return gen_H(sq, order[i], order[i - 1] if i > 0 else None, i + 1 < len(order), desc)

            run1(mkH(0, True))
            if len(order) > 1:
                run1(mkH(1, True))
            for i, sc in enumerate(order):
                slot = sc % 2
                DMA(ht_scr[sc], hT[slot][:, :, :].rearrange("p k t -> p (k t)"), r=hkeys(slot), w=[("hts", sc)])
                rr([gen_fm(slot, range(8), False), gen_tm(slot, 1, 1, False, sc * 2 + 1), gen_tm(slot, 0, 0, False, sc * 2)])
                DMA(xc_scr[sc], xc[:, :, :].rearrange("p g t -> p (g t)"), r=[("xc", g_) for g_ in range(8)], w=[("xcs", sc)])
                DMA(lr_scr[sc], lrT[0:32, :], r=["lrT"], w=[("lrs", sc)])
                ACT(dum[:, 0:1], cst[:, K_ONE:K_ONE + 1], AF.Ln, r=["cst"], w=["dum"])
                rr([chain(gen_ssdA(sq, sc, 1), gen_ssdA(sq, sc, 0)), chain(gen_glaA(sq, sc, 1), gen_glaA(sq, sc, 0)),
                    chain(gen_kvA(sq, sc, 1, l), gen_kvA(sq, sc, 0, l)), mkH(i + 2, True)])
            if sq["kind"] == "p":
                write_states(sq, l, 1)
            order = list(range(nsc))

            def mkH(i, desc):
                if i >= len(order):
                    return None

                def g(sc_):
                    DMA(hT[sc_ % 2][:, :, :].rearrange("p k t -> p (k t)"), ht_scr[sc_], r=[("hts", sc_)], w=hkeys(sc_ % 2))
                    yield
                return g(order[i])
            run1(mkH(0, False))
            if len(order) > 1:
                run1(mkH(1, False))
            pending_out = None
            for i, sc in enumerate(order):
                slot = sc % 2
                DMA(xc[:, :, :].rearrange("p g t -> p (g t)"), xc_scr[sc], r=[("xcs", sc)], w=[("xc", g_) for g_ in range(8)])
                DMA(lrT[0:32, :], lr_scr[sc], r=[("lrs", sc)], w=["lrT"])
                rr([gen_fm(slot, (), True, False), gen_tm(slot, 0, 0, True, sc * 2), gen_tm(slot, 1, 1, True, sc * 2 + 1), pending_out])
                ACT(dum[:, 0:1], cst[:, K_ONE:K_ONE + 1], AF.Ln, r=["cst"], w=["dum"])
                fl = {}
                rr([chain(lane(fl, "c0", gen_ssdB(sq, sc, 0)), gen_ssdB(sq, sc, 1)),
                    chain(lane(fl, "c0", gen_glaB(sq, sc, 0)), gen_glaB(sq, sc, 1)),
                    chain(lane(fl, "c0", gen_attB(sq, sc, 0)), gen_attB(sq, sc, 1)),
                    gated(fl, "c0", 3, gen_out(sq, sc, 0)), mkH(i + 2, False)])
                pending_out = gen_out(sq, sc, 1)
            run1(pending_out)
            if sq["kind"] == "p":
                write_states(sq, l, 0)
    S.emit()
    return nc, len(S.ops)


_PROG_CACHE = {}


def _get_prog(LS, NPR, NL):
    key = (LS, NPR, NL)
    if key not in _PROG_CACHE:
        _PROG_CACHE[key] = build_program(LS, NPR, NL)[0]
    return _PROG_CACHE[key]


def kernel(x_prompt, x_sample, c, cache_k, cache_v, state_ssd, state_gla, c_ctx, w_ada, b_ada,
           norm_w, w_in, conv_w, conv_b, ssd_a_log, ssd_dt_bias, ssd_d, ssd_norm_w, gla_gk_up,
           gla_gk_b, gla_norm_w, attn_q_norm, attn_k_norm, attn_sink, w_out):
    f = lambda a: np.ascontiguousarray(np.asarray(a, dtype=np.float32))
    x_prompt, x_sample = f(x_prompt), f(x_sample)
    NCORE = 8
    B, LP, _ = x_prompt.shape
    NB, LS, _ = x_sample.shape
    NL = w_in.shape[0]
    NPR = B // NCORE
    nc = _get_prog(LS, NPR, NL)
    consts = build_consts()
    rope = build_rope(LS)
    shared = {
        "rope": rope, "consts": consts,
        "w_ada": f(w_ada), "b_ada": f(b_ada), "norm_w": f(norm_w), "w_in": f(w_in), "conv_w": f(conv_w),
        "conv_b": f(conv_b), "a_log": f(ssd_a_log).reshape(NL, 16), "dt_bias": f(ssd_dt_bias).reshape(NL, 16),
        "ssd_d": f(ssd_d), "ssd_norm_w": f(ssd_norm_w), "gk_up": f(gla_gk_up), "gk_b": f(gla_gk_b).reshape(NL, 256),
        "gla_norm_w": f(gla_norm_w), "q_norm": f(attn_q_norm), "k_norm": f(attn_k_norm), "sink": f(attn_sink),
        "w_out": f(w_out),
    }
    c, c_ctx = f(c), f(c_ctx)
    cache_k, cache_v, state_ssd, state_gla = f(cache_k), f(cache_v), f(state_ssd), f(state_gla)
    in_maps = []
    for i in range(NCORE):
        b = i % NB
        m = dict(shared)
        m["xs"] = x_sample[b]
        m["xp"] = x_prompt[i * NPR:(i + 1) * NPR].reshape(NPR * LP, D)
        m["cvec"] = np.ascontiguousarray(np.stack([c[b], c_ctx], 0))
        m["ck"] = np.ascontiguousarray(cache_k[b].reshape(NL, 256, 128))
        m["cv"] = np.ascontiguousarray(cache_v[b].reshape(NL, 256, 128))
        m["sssd"] = np.ascontiguousarray(state_ssd[b].reshape(NL, 2, 512, 128))
        m["sgla"] = np.ascontiguousarray(state_gla[b].reshape(NL, 2, 128, 64))
        in_maps.append(m)
    res = run_bass_kernel_spmd(nc, in_maps, core_ids=list(range(NCORE)))
    R = res.results
    y_sample = np.stack([R[b]["ys"] for b in range(NB)], 0)
    y_prompt = np.concatenate([R[i]["yp"].reshape(NPR, LP, D) for i in range(NCORE)], 0)
    nk = np.concatenate([R[i]["nk"] for i in range(NCORE)], 0).reshape(B, NL, LP, 2, 64)
    nv = np.concatenate([R[i]["nv"] for i in range(NCORE)], 0).reshape(B, NL, LP, 2, 64)
    nssd = np.concatenate([R[i]["nssd"] for i in range(NCORE)], 0).reshape(B, NL, 2, 8, 64, 128)
    ngla = np.concatenate([R[i]["ngla"] for i in range(NCORE)], 0).reshape(B, NL, 2, 4, 32, 64)
    return (y_prompt.astype(np.float32), y_sample.astype(np.float32), nk.astype(np.float32), nv.astype(np.float32),
            nssd.astype(np.float32), ngla.astype(np.float32))
```

```python
import numpy as np
import concourse.bass as bass
import concourse.mybir as mybir
from concourse.bass_utils import run_bass_kernel_spmd

F32 = mybir.dt.float32
BF16 = mybir.dt.bfloat16
AF = mybir.ActivationFunctionType
ALU = mybir.AluOpType
AX = mybir.AxisListType

EPS = 1e-6
D = 1024
DIN = 3120


class _Op:
    __slots__ = ("eng", "fn", "deps", "dma", "signal", "sigval", "dsem", "dval", "dprev")

    def __init__(self, eng, fn, deps, dma):
        self.eng = eng
        self.fn = fn
        self.deps = deps
        self.dma = dma
        self.signal = False
        self.sigval = 0
        self.dsem = None
        self.dval = 0
        self.dprev = 0


class Sched:
    ENGS = ("pe", "act", "dve", "pool", "sp")
    LIM = 30000

    def __init__(self, nc, n_dma_sems=12):
        self.nc = nc
        self.ops = []
        self.writers = {}
        self.readers = {}
        self.K = n_dma_sems
        self.strict = True
        self.attach_wait = True
        self.eng_free = {e: 0.0 for e in self.ENGS}
        self.key_wr = {}
        self.key_rd = {}
        self.last_end = 0.0

    def add(self, eng, fn, r=(), w=(), dma=False, cost=300.0):
        idx = len(self.ops)
        ops = self.ops
        r = list(dict.fromkeys(r))
        w = list(dict.fromkeys(w))
        st = self.eng_free[eng]
        for k in r:
            st = max(st, self.key_wr.get(k, 0.0))
            if isinstance(k, str) and k[:2] in ("fb", "tb"):
                st = max(st, self.key_rd.get(k, 0.0))
        for k in w:
            st = max(st, self.key_wr.get(k, 0.0), self.key_rd.get(k, 0.0))
        en = st + cost
        self.eng_free[eng] = st + 60.0 if dma else en
        for k in w:
            self.key_wr[k] = en
        for k in r:
            self.key_rd[k] = max(self.key_rd.get(k, 0.0), en)
        self.last_end = en
        deps = {}
        for k in r:
            for p in self.writers.get(k, ()):
                deps[p] = True
            if isinstance(k, str) and k[:2] in ("fb", "tb"):
                for p in self.readers.get(k, ()):
                    if ops[p].eng != eng:
                        deps.setdefault(p, False)
        for k in w:
            rd = self.readers.get(k)
            wr = self.writers.get(k)
            if rd:
                for p in rd:
                    deps.setdefault(p, False)
                if wr:
                    for p in wr:
                        deps.setdefault(p, False)
                self.writers[k] = [idx]
                self.readers[k] = []
            elif wr:
                for p in wr:
                    deps.setdefault(p, False)
                nw = [p for p in wr if ops[p].dma or dma or ops[p].eng != eng]
                nw.append(idx)
                self.writers[k] = nw
            else:
                self.writers[k] = [idx]
        for k in r:
            lst = self.readers.setdefault(k, [])
            if not dma:
                lst[:] = [p for p in lst if ops[p].dma or ops[p].eng != eng]
            lst.append(idx)
        deps.pop(idx, None)
        ops.append(_Op(eng, fn, deps, dma))
        return idx

    def emit(self):
        nc = self.nc
        ops = self.ops
        for op in ops:
            nd = {}
            for p, raw in op.deps.items():
                P = ops[p]
                if P.dma:
                    nd[p] = raw
                    continue
                if P.eng == op.eng and not op.dma:
                    if op.eng == "pe" or (not raw and not self.strict):
                        continue
                nd[p] = raw
                P.signal = True
            op.deps = nd
        cnt = {e: 0 for e in self.ENGS}
        dcnt = {e: 0 for e in self.ENGS}
        for op in ops:
            if op.dma:
                n = dcnt[op.eng]
                dcnt[op.eng] = n + 1
                op.dsem = (op.eng, n % self.K)
                op.dval = 16 * (n // self.K + 1)
                op.dprev = 16 * (n // self.K)
            elif op.signal:
                cnt[op.eng] += 1
                op.sigval = cnt[op.eng]
        sems = {}
        for e in self.ENGS:
            for ep in range((cnt[e] + self.LIM - 1) // self.LIM):
                sems[(e, ep)] = nc.alloc_semaphore(f"s_{e}_{ep}")
        dsems = {}
        for e in self.ENGS:
            for j in range(min(self.K, dcnt[e])):
                dsems[(e, j)] = nc.alloc_semaphore(f"d_{e}_{j}")
        LIM = self.LIM

        def sig_of(P):
            ep = (P.sigval - 1) // LIM
            return (P.eng, ep), P.sigval - ep * LIM

        per_eng = {e: [op for op in ops if op.eng == e] for e in self.ENGS}

        def run(ename, eh):
            waited = {}
            dwaited = {}
            for op in per_eng[ename]:
                need = {}
                dneed = {}
                for p in op.deps:
                    P = ops[p]
                    if P.dma:
                        if dwaited.get(P.dsem, 0) < P.dval:
                            dneed[P.dsem] = max(dneed.get(P.dsem, 0), P.dval)
                    else:
                        skey, v = sig_of(P)
                        if waited.get(skey, 0) < v:
                            need[skey] = max(need.get(skey, 0), v)
                if op.dma and op.dprev > 0 and dwaited.get(op.dsem, 0) < op.dprev:
                    dneed[op.dsem] = max(dneed.get(op.dsem, 0), op.dprev)
                wl = [(sems[skey], v) for skey, v in need.items()] + [(dsems[skey], v) for skey, v in dneed.items()]
                for skey, v in need.items():
                    waited[skey] = v
                for skey, v in dneed.items():
                    dwaited[skey] = v
                attach = wl.pop() if (wl and self.attach_wait) else None
                for sm, v in wl:
                    eh.wait_ge(sm, v)
                ins = op.fn(eh)
                if attach is not None:
                    ins._wait_ge(attach[0], attach[1])
                if op.dma:
                    ins.then_inc(dsems[op.dsem], 16)
                elif op.signal:
                    skey, _ = sig_of(op)
                    ins.then_inc(sems[skey], 1)
            for j in range(min(self.K, dcnt[ename])):
                n_on = (dcnt[ename] - 1 - j) // self.K + 1
                if dwaited.get((ename, j), 0) < 16 * n_on:
                    eh.wait_ge(dsems[(ename, j)], 16 * n_on)

        with nc.Block() as block:
            @block.sync
            def _(e):
                run("sp", e)

            @block.tensor
            def _(e):
                run("pe", e)

            @block.scalar
            def _(e):
                run("act", e)

            @block.vector
            def _(e):
                run("dve", e)

            @block.gpsimd
            def _(e):
                run("pool", e)


class _Pool:
    def __init__(self, tiles, names):
        self.tiles = tiles
        self.names = names
        self.free = list(range(len(tiles)))

    def alloc(self):
        assert self.free, "PSUM bank pool exhausted"
        i = self.free.pop(0)
        return i

    def release(self, i):
        assert i not in self.free
        self.free.append(i)


C_XBC, C_GQ, C_GK, C_LR, C_Z, C_B2, C_B3, C_AQ, C_KV, NCOL = 0, 1024, 1152, 1280, 1312, 1824, 2224, 2736, 2992, 3248
WIN_COPIES = [(0, 512, 1024), (1024, 1552, 128), (1152, 1680, 128), (1280, 2320, 32), (1312, 0, 512),
              (1824, 1536, 16), (1840, 1680, 384), (2224, 2064, 256), (2480, 2864, 256),
              (2736, 2352, 256), (2992, 2608, 256)]
K_ID, K_LE, K_GE, K_GT, K_LT, K_ONE, K_BLK4, K_BLKS, NCONST = 0, 128, 256, 384, 512, 640, 768, 1280, 1536


def build_consts():
    c = np.zeros((128, NCONST), np.float32)
    k = np.arange(128)[:, None]
    t = np.arange(128)[None, :]
    c[:, K_ID:K_ID + 128] = (k == t)
    c[:, K_LE:K_LE + 128] = (k <= t)
    c[:, K_GE:K_GE + 128] = (k >= t)
    c[:, K_GT:K_GT + 128] = (k > t)
    c[:, K_LT:K_LT + 128] = (k < t)
    c[:, K_ONE:K_ONE + 128] = 1.0
    blk4 = np.zeros((128, 4, 128), np.float32)
    for h in range(4):
        blk4[32 * h:32 * h + 32, h, :] = 1.0
    c[:, K_BLK4:K_BLK4 + 512] = blk4.reshape(128, 512)
    blks = np.zeros((128, 4, 64), np.float32)
    for h in range(4):
        blks[32 * h:32 * h + 32, h, :] = 1.0
    c[:, K_BLKS:K_BLKS + 256] = blks.reshape(128, 256)
    return c


def build_rope(LS):
    rows = LS // 64
    row_pos = np.repeat(np.arange(rows, dtype=np.float32), 64)
    col_pos = np.tile(np.arange(64, dtype=np.float32), rows)
    inv = (10000.0 ** (-np.arange(16, dtype=np.float32) / 16)).astype(np.float32)
    ar = (row_pos[:, None] * inv[None, :]).astype(np.float32)
    ac = (col_pos[:, None] * inv[None, :]).astype(np.float32)
    cr, sr, cc, sc = np.cos(ar), np.sin(ar), np.cos(ac), np.sin(ac)
    tab = np.concatenate([cr, cr, cc, cc, -sr, sr, -sc, sc], axis=1).astype(np.float32)
    return tab


def build_program(LS, NPR, NL, dump=None):
    LP = 256
    TS = 256
    nc = bass.Bass("TRN2", target_bir_lowering=False)
    S = Sched(nc)
    dump = dump or {}

    def din(name, shape, dt=F32):
        return nc.dram_tensor(name, list(shape), dt, kind="ExternalInput").ap()

    def dout(name, shape, dt=F32):
        return nc.dram_tensor(name, list(shape), dt, kind="ExternalOutput").ap()

    xs_d = din("xs", [LS, D])
    xp_d = din("xp", [NPR * LP, D])
    cvec_d = din("cvec", [2, D])
    ck_d = din("ck", [NL, 256, 128])
    cv_d = din("cv", [NL, 256, 128])
    sssd_d = din("sssd", [NL, 2, 512, 128])
    sgla_d = din("sgla", [NL, 2, 128, 64])
    rope_d = din("rope", [LS, 128])
    consts_d = din("consts", [128, NCONST])
    w_ada_d = din("w_ada", [NL, D, 3 * D])
    b_ada_d = din("b_ada", [NL, 3 * D])
    norm_w_d = din("norm_w", [NL, D])
    w_in_d = din("w_in", [NL, D, DIN])
    conv_w_d = din("conv_w", [NL, 5, 1024])
    conv_b_d = din("conv_b", [NL, 1024])
    a_log_d = din("a_log", [NL, 16])
    dt_bias_d = din("dt_bias", [NL, 16])
    ssd_d_d = din("ssd_d", [NL, 8])
    ssd_nw_d = din("ssd_norm_w", [NL, 512])
    gk_up_d = din("gk_up", [NL, 2, 16, 128])
    gk_b_d = din("gk_b", [NL, 256])
    gla_nw_d = din("gla_norm_w", [NL, 64])
    q_norm_d = din("q_norm", [NL, 64])
    k_norm_d = din("k_norm", [NL, 64])
    sink_d = din("sink", [NL, 4])
    w_out_d = din("w_out", [NL, D, D])

    ys_d = dout("ys", [LS, D])
    yp_d = dout("yp", [NPR * LP, D])
    nk_d = dout("nk", [NPR, NL, LP, 128])
    nv_d = dout("nv", [NPR, NL, LP, 128])
    nssd_d = dout("nssd", [NPR, NL, 2, 512, 128])
    ngla_d = dout("ngla", [NPR, NL, 2, 128, 64])
    NCH_S = LS // 128
    hb_scr = nc.dram_tensor("hb_scr", [NCH_S, 128, 512], BF16, kind="Internal").ap()
    sb_scr = nc.dram_tensor("sb_scr", [NCH_S, 128, 256], BF16, kind="Internal").ap()
    ht_scr = nc.dram_tensor("ht_scr", [max(NCH_S // 2, 1), 128, 8 * 260], BF16, kind="Internal").ap()
    xc_scr = nc.dram_tensor("xc_scr", [max(NCH_S // 2, 1), 128, 8 * 256], BF16, kind="Internal").ap()
    sz_scr = nc.dram_tensor("sz_scr", [max(NCH_S, 2), 128, 512], BF16, kind="Internal").ap()
    sg_scr = nc.dram_tensor("sg_scr", [max(NCH_S, 2), 128, 512], BF16, kind="Internal").ap()
    lr_scr = nc.dram_tensor("lr_scr", [max(NCH_S // 2, 1), 32, 256], F32, kind="Internal").ap()
    kt_scr = nc.dram_tensor("kt_scr", [max(NCH_S, 2), 128, 128], F32, kind="Internal").ap()
    vt_scr = nc.dram_tensor("vt_scr", [max(NCH_S, 2), 128, 256], BF16, kind="Internal").ap()
    dx_scr = nc.dram_tensor("dx_scr", [max(NCH_S, 2), 128, 16], F32, kind="Internal").ap()
    bt_scr = nc.dram_tensor("bt_scr", [max(NCH_S, 2), 128, 256], BF16, kind="Internal").ap()
    dump_d = {n: dout("dbg_" + n, shp) for n, shp in dump.items()}

    def fsz(ap):
        n = 1
        for d in ap.shape[1:]:
            n *= d
        return n

    def vcost(eng, ap):
        n = fsz(ap)
        if eng == "pool":
            return 250.0 + 1.8 * n
        if eng == "act":
            return 280.0 + 0.45 * n
        return 130.0 + 0.85 * n

    def MM(out, lhsT, rhs, start=True, stop=True, r=(), w=()):
        f = 4.0 if lhsT.dtype == F32 else 1.0
        S.add("pe", lambda e: e.matmul(out, lhsT=lhsT, rhs=rhs, start=start, stop=stop), r, w,
              cost=max(fsz(out), 64) / 2.4 * f + 60.0)

    def TR(out, in_, ident, r=(), w=()):
        f = 4.0 if in_.dtype == F32 else 1.0
        S.add("pe", lambda e: e.transpose(out=out, in_=in_, identity=ident), r, w, cost=128 / 2.4 * f + 12.0)

    def ACT(out, in_, func, r=(), w=(), **kw):
        S.add("act", lambda e: e.activation(out=out, in_=in_, func=func, **kw), r, w, cost=vcost("act", out))

    def TT(eng, out, in0, in1, op, r=(), w=()):
        S.add(eng, lambda e: e.tensor_tensor(out=out, in0=in0, in1=in1, op=op), r, w, cost=vcost(eng, out))

    def TS_(eng, out, in0, s1, s2, op0, op1=None, r=(), w=()):
        if op1 is None:
            S.add(eng, lambda e: e.tensor_scalar(out=out, in0=in0, scalar1=s1, scalar2=None, op0=op0), r, w, cost=vcost(eng, out))
        else:
            S.add(eng, lambda e: e.tensor_scalar(out=out, in0=in0, scalar1=s1, scalar2=s2, op0=op0, op1=op1), r, w, cost=vcost(eng, out))

    def STT(eng, out, in0, scalar, in1, op0, op1, r=(), w=()):
        S.add(eng, lambda e: e.scalar_tensor_tensor(out=out, in0=in0, scalar=scalar, in1=in1, op0=op0, op1=op1), r, w, cost=vcost(eng, out))

    def CP(eng, out, in_, r=(), w=()):
        if eng == "act":
            S.add("act", lambda e: e.activation(out=out, in_=in_, func=AF.Copy), r, w, cost=vcost("act", out))
        else:
            S.add(eng, lambda e: e.tensor_copy(out=out, in_=in_), r, w, cost=vcost(eng, out))

    def MSET(eng, ap, val, w=()):
        S.add(eng, lambda e: e.memset(ap, val), (), w)

    def DMA(out, in_, r=(), w=(), q="sp", slow=False):
        c_ = 2000.0 + fsz(out) * 4.0 * 128 / 150.0
        if slow:
            S.add(q, lambda e: e.dma_start(out=out, in_=in_, allow_slow_non_contiguous=True), r, w, dma=True, cost=c_)
        else:
            S.add(q, lambda e: e.dma_start(out=out, in_=in_), r, w, dma=True, cost=c_)

    def DUMP(name, ap, r):
        if name in dump_d:
            DMA(dump_d[name], ap, r=r, w=["dbg_" + name])

    A = nc.alloc_sbuf_tensor

    def rstd_ops(ss_ap, out_ap, n, r, w):
        ACT(out_ap, ss_ap, AF.Ln, r=r, w=w, scale=1.0 / n, bias=EPS)
        ACT(out_ap, out_ap, AF.Exp, r=w, w=w, scale=-0.5)

    fbank = [nc.alloc_psum_tensor(f"fb{i}", [128, 512], F32) for i in range(6)]
    tbank = [nc.alloc_psum_tensor(f"tb{i}", [128, 1024], BF16) for i in range(2)]
    FB = _Pool(fbank, [f"fb{i}" for i in range(6)])
    TB = _Pool(tbank, [f"tb{i}" for i in range(2)])

    cst = A("cst", [128, 1024], F32)
    cstb = A("cstb", [128, 1152], BF16)
    KB_BLK4 = 640
    KF_BLKS = 768
    identf = cst[:, K_ID:K_ID + 128]
    identb = cstb[:, K_ID:K_ID + 128]
    onesf = cst[:, K_ONE:K_ONE + 128]

    def mf(c0):
        return cst[:, c0:c0 + 128]

    def mb(c0):
        return cstb[:, c0:c0 + 128]

    win = A("win", [128, 8, NCOL], BF16)
    wout = A("wout", [128, 8, D], BF16)
    gate_bc = A("gate_bc", [128, 2, D], BF16)
    scT_t = A("scT", [128, 16], F32)
    scT = scT_t[:, :].rearrange("p (c k) -> p c k", c=2)
    modT = A("modT", [128, 24, 2], F32)
    prow = A("prow", [128, 128], F32)
    colsT = A("colsT", [128, 128], F32)
    badaT = colsT[:, 0:24]
    dg = A("dg", [128, 128], F32)
    normwT = colsT[:, 24:32]
    g1T = A("g1T", [128, 8, 2], F32)
    shT = A("shT", [128, 8, 2], F32)
    cwT = colsT[:, 32:72].rearrange("p (t g) -> p t g", t=5)
    cbT = colsT[:, 72:80]
    A_bc = A("A_bc", [128, 16], F32)
    dtb_bc = A("dtb_bc", [128, 16], F32)
    D_bc = A("D_bc", [128, 8], F32)
    rsT = A("rsT", [128, 8], F32)
    qkw_bc = A("qkw_bc", [128, 6, 64], F32)
    gkup = A("gkup", [33, 256], F32)
    esink = A("esink", [128, 4], F32)

    xtH_t = A("xtH", [128, 2, D], F32)
    xtH = [xtH_t[:, 0, :], xtH_t[:, 1, :]]
    stage = xtH_t[:, :, :].rearrange("p j d -> p (j d)")
    STG = ["xtH0", "xtH1"]
    xtO = [A(f"xtO{i}", [128, D], F32) for i in range(2)]
    DMA(stage[:, 0:NCONST], consts_d, w=STG)
    CP("dve", cst[:, 0:768], stage[:, 0:768], r=STG, w=["cst"])
    CP("dve", cst[:, 768:1024], stage[:, K_BLKS:K_BLKS + 256], r=STG, w=["cst"])
    CP("dve", cstb[:, 0:640], stage[:, 0:640], r=STG, w=["cstb"])
    CP("dve", cstb[:, 640:1152], stage[:, K_BLK4:K_BLK4 + 512], r=STG, w=["cstb"])
    MSET("dve", prow[:], 0.0, w=["prow"])
    for c in range(2):
        DMA(prow[c * 8:(c + 1) * 8, :], cvec_d[c].rearrange("(k p) -> k p", p=128), w=["prow"])
    pb0 = FB.alloc()
    TR(fbank[pb0][:, 0:128], prow[:], identf, r=["prow", "cst"], w=[FB.names[pb0]])
    ACT(scT_t[:, :], fbank[pb0][:, 0:16], AF.Silu, r=[FB.names[pb0]], w=["scT"])
    FB.release(pb0)
    MSET("dve", rsT[:], 1.0, w=["rsT"])
    MSET("dve", gkup[:], 0.0, w=["gkup"])

    hT = [A(f"hT{i}", [128, 8, TS + 4], BF16) for i in range(2)]
    xh = A("xh", [128, D], BF16)
    ss2 = A("ss2", [128, 2], F32)
    rs2 = A("rs2", [128, 2], F32)
    xpre = A("xpre", [128, 3, TS + 4], BF16)
    cdiag = A("cdiag", [128, 5, 8, 128], BF16)
    xc = A("xc", [128, 8, TS], BF16)
    gqk = A("gqk", [128, 2, TS], F32)
    lrT = A("lrT", [33, TS], F32)
    MSET("dve", lrT[32:33, :], 1.0, w=["lrT1"])
    sz = [A(f"sz{i}", [128, 512], BF16) for i in range(2)]
    sg = [A(f"sg{i}", [128, 512], BF16) for i in range(2)]
    dtx = [A(f"dtx{i}", [128, 16], F32) for i in range(2)]
    dte = A("dte", [128, 16], F32)
    dtt = A("dtt", [128, 16], F32)
    aa = A("aa", [128, 16], F32)
    dum = A("dum", [128, 1], F32)
    ecs = A("ecs", [128, 32], F32)
    mdt = A("mdt", [128, 8], F32)
    ktok = [A(f"ktok{i}", [128, 128], F32) for i in range(2)]
    vtok = [A(f"vtok{i}", [128, 256], BF16) for i in range(2)]
    Lh = A("Lh", [128, 16, 128], BF16)
    eseg = A("eseg", [128, 16, 128], BF16)
    GM = A("GM", [128, 2, 2, 128], BF16)
    xdt = A("xdt", [128, 2, 512], BF16)
    xdec = A("xdec", [128, 512], BF16)
    diagD = A("diagD", [128, 4, 128], BF16)
    Dcol = A("Dcol", [128, 4], F32)
    Btok = A("Btok", [128, 256], BF16)
    Hf = A("Hf", [128, 512], F32)
    Hb = A("Hb", [128, 512], F32)
    Hfb = A("Hfb", [128, 512], BF16)
    Hio = A("Hio", [128, 512], BF16)
    ysb = A("ysb", [128, 512], F32)
    sti = ysb[:, :].rearrange("p (b n) -> p b n", b=4)
    yt2 = A("yt2", [128, 512], F32)
    xo = A("xo", [128, 512], F32)
    ssy = A("ssy", [128, 1], F32)
    rsy = A("rsy", [128, 1], F32)
    ymix = [A(f"ymix{i}", [128, D], BF16) for i in range(2)]
    ymT = A("ymT", [128, 8, 128], BF16)
    sp_ = A("sp_", [128, 256], F32)
    egq = A("egq", [128, 2, 128], F32)
    egk = A("egk", [128, 2, 128], F32)
    ekd = A("ekd", [128, 128], F32)
    egt = A("egt", [128, 2], F32)
    qin = A("qin", [128, 2, 128], BF16)
    kin = A("kin", [128, 2, 128], BF16)
    Qblk = A("Qblk", [128, 2, 512], BF16)
    attm = A("attm", [128, 2, 512], BF16)
    kdec = A("kdec", [128, 128], BF16)
    Sf = A("Sf", [128, 256], F32)
    Sb = A("Sb", [128, 256], F32)
    Sfb = A("Sfb", [128, 256], BF16)
    Sio = A("Sio", [128, 256], BF16)
    stmp = A("stmp", [128, 256], F32)
    sso = A("sso", [128, 4], F32)
    rso = A("rso", [128, 4], F32)
    og = A("og", [128, 256], F32)
    oat = A("oat", [128, 256], F32)
    ckst = og[:, :].rearrange("p (b f) -> p b f", b=2)
    ropet = [A(f"ropet{i}", [128, 128], F32) for i in range(2)]
    ssq = A("ssq", [128, 4], F32)
    rsq = A("rsq", [128, 4], F32)
    sj = A("sj", [128, 512], BF16)
    qn = A("qn", [128, 256], F32)
    raw = [A(f"raw{i}", [128, 256], F32) for i in range(2)]
    qr1 = A("qr1", [128, 256], F32)
    qr2 = A("qr2", [128, 256], F32)
    qb = A("qb", [128, 256], BF16)
    qT = A("qT", [128, 2, 128], BF16)
    knb = A("knb", [128, 128], BF16)
    NCHMAX = max(LS, LP) // 128
    KT = A("KT", [128, NCHMAX * 128], BF16)
    Vt = A("Vt", [128, NCHMAX, 2, 65], BF16)
    KTc = A("KTc", [128, 256], BF16)
    Vc = A("Vc", [128, 2, 2, 65], BF16)
    ckb = A("ckb", [128, 2, 128], BF16)
    PT = A("PT", [128, 5, 256], BF16)
    den = A("den", [128, 4], F32)
    MSET("dve", Vt[:, :, :, 64:65], 1.0, w=["Vt1"])
    MSET("dve", Vc[:, :, :, 64:65], 1.0, w=["Vc1"])

    def galloc(pool):
        while not pool.free:
            yield "blocked"
        return pool.alloc()

    def rr(gens):
        gens = [[g, 0.0] for g in gens if g is not None]
        blocked = set()
        nblocked = 0
        while gens:
            cand = [x for x in gens if id(x) not in blocked]
            if not cand:
                blocked.clear()
                cand = gens
            x = min(cand, key=lambda y: y[1])
            try:
                v = next(x[0])
            except StopIteration:
                gens.remove(x)
                blocked.clear()
                nblocked = 0
                continue
            if v == "blocked":
                blocked.add(id(x))
                nblocked += 1
                if nblocked > 4 * len(gens) + 4:
                    raise RuntimeError("emission schedule deadlock (PSUM pool / flags)")
            else:
                x[1] = S.last_end
                blocked.clear()
                nblocked = 0

    def chain(*gs):
        for g in gs:
            if g is not None:
                yield from g

    def lane(flags, key, g):
        yield from g
        flags[key] = flags.get(key, 0) + 1

    def gated(flags, key, need, g):
        while flags.get(key, 0) < need:
            yield "blocked"
        yield from g

    def run1(g):
        rr([g])

    def load_layer(l):
        stg = [(xtH[0], "xtH0"), (xtH[1], "xtH1"), (xtO[0][:, :], "xtO0"), (xtO[1][:, :], "xtO1")]
        cnt = [0]

        def nxt():
            t_ = stg[cnt[0] % 4]
            cnt[0] += 1
            return t_
        DMA(prow[0:24, :], b_ada_d[l].rearrange("(j p) -> j p", p=128), w=["prow"])
        DMA(prow[24:32, :], norm_w_d[l].rearrange("(j p) -> j p", p=128), w=["prow"])
        DMA(prow[32:72, :], conv_w_d[l].rearrange("t (g p) -> (t g) p", p=128), w=["prow"])
        DMA(prow[72:80, :], conv_b_d[l].rearrange("(j p) -> j p", p=128), w=["prow"])
        DMA(prow[80:84, :], ssd_nw_d[l].rearrange("(j p) -> j p", p=128), w=["prow"])
        pbc = FB.alloc()
        TR(fbank[pbc][:, 0:128], prow[:], identf, r=["prow", "cst"], w=[FB.names[pbc]])
        CP("dve", colsT[:], fbank[pbc][:, 0:128], r=[FB.names[pbc]], w=["colsT"])
        FB.release(pbc)
        CP("dve", rsT[:, 0:4], colsT[:, 80:84], r=["colsT"], w=["rsT"])
        for t in range(5):
            for g in range(8):
                if (t * 8 + g) % 2 == 0:
                    TS_("dve", cdiag[:, t, g, :], identf, cwT[:, t, g:g + 1], None, ALU.mult, r=["cst", "colsT"], w=["cdiag"])
                else:
                    ACT(cdiag[:, t, g, :], identf, AF.Copy, r=["cst", "colsT"], w=["cdiag"], scale=cwT[:, t, g:g + 1])
        DMA(A_bc[:], a_log_d[l:l + 1, :].partition_broadcast(128), w=["A_bc"])
        ACT(A_bc[:], A_bc[:], AF.Exp, r=["A_bc"], w=["A_bc"])
        TS_("dve", A_bc[:], A_bc[:], -1.0, None, ALU.mult, r=["A_bc"], w=["A_bc"])
        DMA(dtb_bc[:], dt_bias_d[l:l + 1, :].partition_broadcast(128), w=["dtb_bc"])
        DMA(D_bc[:], ssd_d_d[l:l + 1, :].partition_broadcast(128), w=["D_bc"])
        dv_ = D_bc[:, :].rearrange("p (g two) -> p g two", two=2)
        CP("dve", Dcol[0:64, :], dv_[0:64, :, 0], r=["D_bc"], w=["Dcol"])
        CP("dve", Dcol[64:128, :], dv_[64:128, :, 1], r=["D_bc"], w=["Dcol"])
        for g in range(4):
            TS_("dve", diagD[:, g, :], identf, Dcol[:, g:g + 1], None, ALU.mult, r=["cst", "Dcol"], w=["diagD"])
        for half in range(2):
            for kk in (4, 5):
                DMA(rsT[half * 64:(half + 1) * 64, kk:kk + 1], gla_nw_d[l].rearrange("(p o) -> p o", o=1), w=["rsT"])
        for hh in range(6):
            src = q_norm_d if hh < 4 else k_norm_d
            DMA(qkw_bc[:, hh, :], src[l:l + 1, :].partition_broadcast(128), w=["qkw_bc"])
        DMA(gkup[0:16, 0:128], gk_up_d[l, 0], w=["gkup"])
        DMA(gkup[16:32, 128:256], gk_up_d[l, 1], w=["gkup"])
        DMA(gkup[32:33, :], gk_b_d[l:l + 1, :], w=["gkup"])
        DMA(esink[:], sink_d[l:l + 1, :].partition_broadcast(128), w=["esink"])
        ACT(esink[:], esink[:], AF.Exp, r=["esink"], w=["esink"])
        def emit_win(k, c_lo, c_hi):
            st_, sk_ = nxt()
            DMA(st_[:, 0:c_hi - c_lo], w_in_d[l, k * 128:(k + 1) * 128, c_lo:c_hi], w=[sk_])
            for ci, (dst, src, n) in enumerate(WIN_COPIES):
                if not (c_lo <= src and src + n <= c_hi):
                    continue
                eng = ("dve", "act", "dve", "act", "pool")[(ci + k) % 5]
                CP(eng, win[:, k, dst:dst + n], st_[:, src - c_lo:src - c_lo + n], r=[sk_], w=["win"])

        pm = FB.alloc()

        def emit_ada(c3, k):
            st_, sk_ = nxt()
            DMA(st_[:, 0:1024], w_ada_d[l, k * 128:(k + 1) * 128, c3 * 1024:(c3 + 1) * 1024], w=[sk_])
            for jj in range(8):
                jg = c3 * 8 + jj
                MM(fbank[pm][:, jg * 2:jg * 2 + 2], lhsT=st_[:, jj * 128:(jj + 1) * 128], rhs=scT[:, :, k],
                   start=(c3 == 0 and k == 0 and jj == 0), stop=(c3 == 2 and k == 7 and jj == 7),
                   r=[sk_, "scT"], w=[FB.names[pm]])

        win_pieces = [(k, lo, hi) for k in range(8) for (lo, hi) in ((0, 512), (512, 1536), (1536, 2352), (2352, DIN))]
        ada_pieces = [(c3, k) for c3 in range(3) for k in range(8)]
        for i_ in range(len(win_pieces)):
            emit_win(*win_pieces[i_])
            if i_ < len(ada_pieces):
                emit_ada(*ada_pieces[i_])
        CP("dve", modT[:].rearrange("p j c -> p (j c)"), fbank[pm][:, 0:48], r=[FB.names[pm]], w=["modT"])
        FB.release(pm)
        TT("dve", modT[:], modT[:], badaT[:].unsqueeze(2).to_broadcast([128, 24, 2]), ALU.add, r=["modT", "colsT"], w=["modT"])
        CP("dve", shT[:], modT[:, 0:8, :], r=["modT"], w=["shT"])
        TS_("dve", g1T[:], modT[:, 8:16, :], 1.0, None, ALU.add, r=["modT"], w=["g1T"])
        TT("dve", g1T[:], g1T[:], normwT[:].unsqueeze(2).to_broadcast([128, 8, 2]), ALU.mult, r=["g1T", "colsT"], w=["g1T"])
        for c in range(2):
            for k in range(8):
                TS_("dve", dg[:], identf, modT[:, 16 + k, c:c + 1], None, ALU.mult, r=["cst", "modT"], w=["dg"])
                pg = FB.alloc()
                MM(fbank[pg][:, 0:128], lhsT=onesf, rhs=dg[:], r=["cst", "dg"], w=[FB.names[pg]])
                CP("act", gate_bc[:, c, k * 128:(k + 1) * 128], fbank[pg][:, 0:128], r=[FB.names[pg]], w=["gate_bc"])
                FB.release(pg)
        for k in range(8):
            st_, sk_ = nxt()
            DMA(st_[:, 0:D], w_out_d[l, k * 128:(k + 1) * 128, :], w=[sk_])
            if k % 2 == 0:
                TS_("dve", wout[:, k, :], st_[:, 0:D], rsT[:, k:k + 1], None, ALU.mult, r=[sk_, "rsT"], w=["wout"])
            else:
                ACT(wout[:, k, :], st_[:, 0:D], AF.Copy, r=[sk_, "rsT"], w=["wout"], scale=rsT[:, k:k + 1])

    def gen_H(sq, sc, prev_sc, has_next, descending):
        slot = sc % 2
        c = sq["cond"]
        kx = ("X", sq["id"], sc)
        for j in range(2):
            ch = sc * 2 + j
            DMA(xtH[j], sq["src"][ch * 128:(ch + 1) * 128, :], r=[kx], w=[f"xtH{j}"])
            ACT(xh[:], xtH[j], AF.Square, r=[f"xtH{j}"], w=["xh", "ss2"], accum_out=ss2[:, j:j + 1])
        rstd_ops(ss2[:], rs2[:], D, r=["ss2"], w=["rs2"])
        yield
        for j in range(2):
            TS_("dve", xh[:], xtH[j], rs2[:, j:j + 1], None, ALU.mult, r=[f"xtH{j}", "rs2"], w=["xh"])
            tb = yield from galloc(TB)
            for k in range(8):
                TR(tbank[tb][:, k * 128:(k + 1) * 128], xh[:, k * 128:(k + 1) * 128], identb, r=["xh", "cstb"], w=[TB.names[tb]])
            hv = hT[slot][:, :, 2 + j * 128:2 + (j + 1) * 128]
            TT("dve", hv, tbank[tb][:, :].rearrange("p (k t) -> p k t", k=8), g1T[:, :, c:c + 1].to_broadcast([128, 8, 128]), ALU.mult,
               r=[TB.names[tb], "g1T"], w=[("hT", slot, "m")])
            TB.release(tb)
            TT("pool", hv, hv, shT[:, :, c:c + 1].to_broadcast([128, 8, 128]), ALU.add, r=[("hT", slot, "m"), "shT"], w=[("hT", slot, "m")])
            yield
        lcols, rcols = slice(0, 2), slice(TS + 2, TS + 4)
        if prev_sc is None:
            first_side = "r" if descending else "l"
            MSET("pool", hT[slot][:, :, rcols if first_side == "r" else lcols], 0.0, w=[("hT", slot, first_side)])
        else:
            ps = prev_sc % 2
            if prev_sc > sc:
                CP("pool", hT[slot][:, :, rcols], hT[ps][:, :, 2:4], r=[("hT", ps, "m")], w=[("hT", slot, "r")])
                CP("pool", hT[ps][:, :, lcols], hT[slot][:, :, TS:TS + 2], r=[("hT", slot, "m")], w=[("hT", ps, "l")])
            else:
                CP("pool", hT[slot][:, :, lcols], hT[ps][:, :, TS:TS + 2], r=[("hT", ps, "m")], w=[("hT", slot, "l")])
                CP("pool", hT[ps][:, :, rcols], hT[slot][:, :, 2:4], r=[("hT", slot, "m")], w=[("hT", ps, "r")])
        if not has_next:
            last_side = "l" if descending else "r"
            MSET("pool", hT[slot][:, :, lcols if last_side == "l" else rcols], 0.0, w=[("hT", slot, last_side)])

    def hkeys(slot):
        return [("hT", slot, "m"), ("hT", slot, "l"), ("hT", slot, "r")]

    def gen_fm(slot, groups, with_qk, with_lr=True):
        def conv_part(g, xs_):
            pc_ = yield from galloc(FB)
            for t in range(5):
                MM(fbank[pc_][:, 0:TS], lhsT=cdiag[:, t, g, :], rhs=xpre[:, xs_, t:t + TS], start=(t == 0), stop=(t == 4),
                   r=["cdiag", ("xpre", xs_)], w=[FB.names[pc_]])
            ACT(xc[:, g, :], fbank[pc_][:, 0:TS], AF.Silu, r=[FB.names[pc_], "colsT"], w=[("xc", g)], bias=cbT[:, g:g + 1])
            FB.release(pc_)

        pend = None
        for gi, g in enumerate(groups):
            b = yield from galloc(FB)
            for k in range(8):
                MM(fbank[b][:, 0:TS + 4], lhsT=win[:, k, C_XBC + g * 128:C_XBC + (g + 1) * 128], rhs=hT[slot][:, k, :],
                   start=(k == 0), stop=(k == 7), r=["win"] + hkeys(slot), w=[FB.names[b]])
            xs_ = gi % 3
            CP("act", xpre[:, xs_, :], fbank[b][:, 0:TS + 4], r=[FB.names[b]], w=[("xpre", xs_)])
            FB.release(b)
            if pend is not None:
                yield from conv_part(*pend)
            pend = (g, xs_)
            yield
        if pend is not None:
            yield from conv_part(*pend)
            yield
        if with_qk:
            for which, c0 in ((0, C_GQ), (1, C_GK)):
                b = yield from galloc(FB)
                for k in range(8):
                    MM(fbank[b][:, 0:TS], lhsT=win[:, k, c0:c0 + 128], rhs=hT[slot][:, k, 2:2 + TS],
                       start=(k == 0), stop=(k == 7), r=["win", ("hT", slot, "m")], w=[FB.names[b]])
                CP("act", gqk[:, which, :], fbank[b][:, 0:TS], r=[FB.names[b]], w=["gqk"])
                FB.release(b)
                yield
        if not with_lr:
            return
        b = yield from galloc(FB)
        for k in range(8):
            MM(fbank[b][0:32, 0:TS], lhsT=win[:, k, C_LR:C_LR + 32], rhs=hT[slot][:, k, 2:2 + TS],
               start=(k == 0), stop=(k == 7), r=["win", ("hT", slot, "m")], w=[FB.names[b]])
        CP("act", lrT[0:32, :], fbank[b][0:32, 0:TS], r=[FB.names[b]], w=["lrT"])
        FB.release(b)

    def tm_mm(b, slot, j, c0, n):
        for k in range(8):
            MM(fbank[b][:, 0:n], lhsT=hT[slot][:, k, 2 + j * 128:2 + (j + 1) * 128], rhs=win[:, k, c0:c0 + n],
               start=(k == 0), stop=(k == 7), r=["win", ("hT", slot, "m")], w=[FB.names[b]])

    def gen_tm(slot, j, p, full, ch=None):
        b = yield from galloc(FB)
        tm_mm(b, slot, j, C_AQ if full else C_KV, 256)
        CP("act", raw[p][:], fbank[b][:, 0:256], r=[FB.names[b]], w=[f"raw{p}"])
        FB.release(b)
        yield
        if not full:
            b = yield from galloc(FB)
            tm_mm(b, slot, j, C_Z, 512)
            ACT(sz[p][:], fbank[b][:, :], AF.Silu, r=[FB.names[b]], w=[f"sz{p}"])
            FB.release(b)
            DMA(sz_scr[ch], sz[p][:], r=[f"sz{p}"], w=[("szs", ch)])
            yield
            b = yield from galloc(FB)
            tm_mm(b, slot, j, C_B3, 512)
            ACT(sg[p][:], fbank[b][:, :], AF.Silu, r=[FB.names[b]], w=[f"sg{p}"])
            FB.release(b)
            DMA(sg_scr[ch], sg[p][:], r=[f"sg{p}"], w=[("sgs", ch)])
            yield
        if full:
            DMA(sz[p][:], sz_scr[ch], r=[("szs", ch)], w=[f"sz{p}"])
            DMA(sg[p][:], sg_scr[ch], r=[("sgs", ch)], w=[f"sg{p}"])
            DMA(dtx[p][:], dx_scr[ch], r=[("dxs", ch)], w=[f"dtx{p}"])
            DMA(ktok[p][:], kt_scr[ch], r=[("kts", ch)], w=[f"ktok{p}"])
            DMA(vtok[p][:], vt_scr[ch], r=[("vts", ch)], w=[f"vtok{p}"])
            return
        b = yield from galloc(FB)
        tm_mm(b, slot, j, C_B2, 400)
        TT("dve", dtx[p][:], fbank[b][:, 0:16], dtb_bc[:], ALU.add, r=[FB.names[b], "dtb_bc"], w=[f"dtx{p}"])
        CP("act", ktok[p][:], fbank[b][:, 16:144], r=[FB.names[b]], w=[f"ktok{p}"])
        CP("act", vtok[p][:], fbank[b][:, 144:400], r=[FB.names[b]], w=[f"vtok{p}"])
        FB.release(b)
        DMA(dx_scr[ch], dtx[p][:], r=[f"dtx{p}"], w=[("dxs", ch)])
        DMA(kt_scr[ch], ktok[p][:], r=[f"ktok{p}"], w=[("kts", ch)])
        DMA(vt_scr[ch], vtok[p][:], r=[f"vtok{p}"], w=[("vts", ch)])

    def softplus_dt(p):
        ACT(dte[:], dtx[p][:], AF.Exp, r=[f"dtx{p}"], w=["dte"])
        ACT(dtt[:], dte[:], AF.Ln, r=["dte"], w=["dtt"], bias=1.0)
        TT("dve", aa[:], dtt[:], A_bc[:], ALU.mult, r=["dtt", "A_bc"], w=["aa"])

    def gates_sp(b, j, cols):
        MM(fbank[b][:, 0:256], lhsT=lrT[0:33, j * 128:(j + 1) * 128], rhs=gkup[0:33, :], r=["lrT", "lrT1", "gkup"], w=[FB.names[b]])
        ACT(sp_[:, cols], fbank[b][:, cols], AF.Exp, r=[FB.names[b]], w=["sp_"], scale=-1.0)
        FB.release(b)
        TS_("dve", sp_[:, cols], sp_[:, cols], 1e30, None, ALU.min, r=["sp_"], w=["sp_"])
        ACT(sp_[:, cols], sp_[:, cols], AF.Ln, r=["sp_"], w=["sp_"], bias=1.0)

    def xs_tokmajor(tb, j):
        for g in range(4):
            TR(tbank[tb][:, g * 128:(g + 1) * 128], xc[:, g, j * 128:(j + 1) * 128], identb, r=[("xc", g), "cstb"], w=[TB.names[tb]])

    def b_tokmajor(tb, j):
        for g in range(2):
            TR(tbank[tb][:, g * 128:(g + 1) * 128], xc[:, 4 + g, j * 128:(j + 1) * 128], identb, r=[("xc", 4 + g), "cstb"], w=[TB.names[tb]])
        CP("act", Btok[:], tbank[tb][:, 0:256], r=[TB.names[tb]], w=["Btok"])
        TB.release(tb)

    def bc8(ap):
        return ap.unsqueeze(2).to_broadcast([128, 8, 64])

    def v3(ap, h=8):
        return ap.rearrange("p (h e) -> p h e", h=h)

    def rope_apply(src, dst, H, rt, rk, r_src, w_dst):
        x5 = src.rearrange("p (h a b e) -> p h a b e", h=H, a=2, b=2)
        cosb = rt[:, 0:64].unsqueeze(1).to_broadcast([128, H, 64])
        s4 = rt[:, 64:128].rearrange("p (a b e) -> p a b e", a=2, b=2)
        TT("dve", v3(qr1[:, 0:H * 64], H), v3(src, H), cosb, ALU.mult, r=r_src + [rk], w=["qr1"])
        q5 = qr2[:, 0:H * 64].rearrange("p (h a b e) -> p h a b e", h=H, a=2, b=2)
        TT("dve", q5[:, :, :, 0, :], x5[:, :, :, 1, :], s4[:, :, 0, :].unsqueeze(1).to_broadcast([128, H, 2, 16]), ALU.mult,
           r=r_src + [rk], w=["qr2"])
        TT("dve", q5[:, :, :, 1, :], x5[:, :, :, 0, :], s4[:, :, 1, :].unsqueeze(1).to_broadcast([128, H, 2, 16]), ALU.mult,
           r=r_src + [rk], w=["qr2"])
        TT("dve", dst, qr1[:, 0:H * 64], qr2[:, 0:H * 64], ALU.add, r=["qr1", "qr2"], w=w_dst)

    def gen_ssdA(sq, sc, j):
        ch = sc * 2 + j
        p = ch % 2
        softplus_dt(p)
        pw = yield from galloc(FB)
        MM(fbank[pw][:, 0:8], lhsT=mf(K_LT), rhs=aa[:, 8:16], r=["cst", "aa"], w=[FB.names[pw]])
        MM(fbank[pw][:, 8:16], lhsT=onesf, rhs=aa[:, 8:16], r=["cst", "aa"], w=[FB.names[pw]])
        ACT(ecs[:, 0:16], fbank[pw][:, 0:16], AF.Exp, r=[FB.names[pw]], w=["ecs"])
        FB.release(pw)
        TT("dve", mdt[:], dtt[:, 8:16], ecs[:, 0:8], ALU.mult, r=["dtt", "ecs"], w=["mdt"])
        yield
        tx = yield from galloc(TB)
        xs_tokmajor(tx, j)
        TT("dve", v3(xdec[:]), v3(tbank[tx][:, 0:512]), bc8(mdt[:]), ALU.mult, r=[TB.names[tx], "mdt"], w=["xdec"])
        TB.release(tx)
        yield
        tb = yield from galloc(TB)
        b_tokmajor(tb, j)
        DMA(bt_scr[ch], Btok[:], r=["Btok"], w=[("bts", ch)])
        yield
        pst = yield from galloc(FB)
        for g in range(2):
            MM(fbank[pst][:, g * 256:(g + 1) * 256], lhsT=Btok[:, g * 128:(g + 1) * 128], rhs=xdec[:, g * 256:(g + 1) * 256],
               r=["Btok", "xdec"], w=[FB.names[pst]])
        CP("dve", Hio[:], Hb[:], r=["Hb"], w=["Hio"])
        DMA(hb_scr[ch], Hio[:], r=["Hio"], w=[("hbs", ch)])
        TT("dve", v3(Hb[:]), v3(Hb[:]), bc8(ecs[:, 8:16]), ALU.mult, r=["Hb", "ecs"], w=["Hb"])
        TT("dve", Hb[:], Hb[:], fbank[pst][:, :], ALU.add, r=["Hb", FB.names[pst]], w=["Hb"])
        FB.release(pst)

    def gen_glaA(sq, sc, j):
        ch = sc * 2 + j
        p = ch % 2
        b = yield from galloc(FB)
        gates_sp(b, j, slice(128, 256))
        yield
        pg = yield from galloc(FB)
        MM(fbank[pg][:, 0:128], lhsT=mf(K_LT), rhs=sp_[:, 128:256], r=["cst", "sp_"], w=[FB.names[pg]])
        MM(fbank[pg][:, 128:129], lhsT=sp_[:, 128:256], rhs=cst[:, K_ONE:K_ONE + 1], r=["cst", "sp_"], w=[FB.names[pg]])
        ACT(ekd[:], fbank[pg][:, 0:128], AF.Exp, r=[FB.names[pg]], w=["ekd"], scale=-1.0 / 16)
        ACT(egt[:, 1:2], fbank[pg][:, 128:129], AF.Exp, r=[FB.names[pg]], w=["egt"], scale=-1.0 / 16)
        FB.release(pg)
        TT("dve", kdec[:], ktok[p][:], ekd[:], ALU.mult, r=[f"ktok{p}", "ekd"], w=["kdec"])
        yield
        pS = yield from galloc(FB)
        MM(fbank[pS][:, 0:256], lhsT=kdec[:], rhs=vtok[p][:], r=["kdec", f"vtok{p}"], w=[FB.names[pS]])
        CP("dve", Sio[:], Sb[:], r=["Sb"], w=["Sio"])
        DMA(sb_scr[ch], Sio[:], r=["Sio"], w=[("sbs", ch)])
        TT("dve", stmp[:], fbank[pS][:, 0:256], cst[:, KF_BLKS:KF_BLKS + 256], ALU.mult, r=[FB.names[pS], "cst"], w=["stmp"])
        FB.release(pS)
        STT("dve", Sb[:], Sb[:], egt[:, 1:2], stmp[:], ALU.mult, ALU.add, r=["Sb", "egt", "stmp"], w=["Sb"])

    def gen_kvA(sq, sc, j, l):
        ch = sc * 2 + j
        slot = sc % 2
        sample = sq["kind"] == "s"
        if sample:
            DMA(ropet[j][:], rope_d[ch * 128:(ch + 1) * 128, :], w=[f"ropet{j}"])
        p = ch % 2
        rw, rk_ = raw[p], f"raw{p}"
        for h in range(2):
            ACT(sj[:, 256 + h * 64:256 + (h + 1) * 64], rw[:, h * 64:(h + 1) * 64], AF.Square, r=[rk_], w=["sja", "ssq"],
                accum_out=ssq[:, h:h + 1])
        rstd_ops(ssq[:, 0:2], rsq[:, 0:2], 64, r=["ssq"], w=["rsq"])
        for h in range(2):
            STT("dve", qn[:, h * 64:(h + 1) * 64], rw[:, h * 64:(h + 1) * 64], rsq[:, h:h + 1], qkw_bc[:, 4 + h, :],
                ALU.mult, ALU.mult, r=[rk_, "rsq", "qkw_bc"], w=["qn"])
        CP("pool", Vt[:, ch, :, 0:64], rw[:, 128:256].rearrange("p (j e) -> p j e", j=2), r=[rk_], w=[("Vt", ch)])
        if not sample:
            DMA(nk_d[sq["pi"], l, j * 128:(j + 1) * 128, :], qn[:, 0:128], r=["qn"], w=[("nk", sq["pi"], l, j)])
            DMA(nv_d[sq["pi"], l, j * 128:(j + 1) * 128, :], rw[:, 128:256], r=[rk_], w=[("nv", sq["pi"], l, j)])
        yield
        if sample:
            rope_apply(qn[:, 0:128], knb[:], 2, ropet[j], f"ropet{j}", ["qn"], ["knb"])
        else:
            CP("dve", knb[:], qn[:, 0:128], r=["qn"], w=["knb"])
        yield
        tb = yield from galloc(TB)
        TR(tbank[tb][:, 0:128], knb[:], identb, r=["knb", "cstb"], w=[TB.names[tb]])
        CP("act", KT[:, ch * 128:(ch + 1) * 128], tbank[tb][:, 0:128], r=[TB.names[tb]], w=[("KT", ch)])
        TB.release(tb)

    def gen_ssdB(sq, sc, j):
        ch = sc * 2 + j
        p = ch % 2
        js = slice(j * 128, (j + 1) * 128)
        AA, DT = aa, dtt
        ka, kd = "aa", "dtt"
        DMA(Hio[:], hb_scr[ch], r=[("hbs", ch)], w=["Hio"])
        softplus_dt(p)
        pc = yield from galloc(FB)
        MM(fbank[pc][:, 0:8], lhsT=mf(K_LE), rhs=AA[:, 0:8], r=["cst", ka], w=[FB.names[pc]])
        MM(fbank[pc][:, 8:16], lhsT=mf(K_GE), rhs=AA[:, 8:16], r=["cst", ka], w=[FB.names[pc]])
        MM(fbank[pc][:, 16:24], lhsT=mf(K_GT), rhs=AA[:, 0:8], r=["cst", ka], w=[FB.names[pc]])
        MM(fbank[pc][:, 24:32], lhsT=onesf, rhs=AA[:, 0:8], r=["cst", ka], w=[FB.names[pc]])
        ACT(ecs[:, 0:32], fbank[pc][:, 0:32], AF.Exp, r=[FB.names[pc]], w=["ecs"])
        FB.release(pc)
        for d_, mk in ((0, K_GT), (1, K_LT)):
            TT("dve", Lh[:, d_ * 8:(d_ + 1) * 8, :], mb(mk).unsqueeze(1).to_broadcast([128, 8, 128]),
               AA[:, d_ * 8:(d_ + 1) * 8].unsqueeze(2).to_broadcast([128, 8, 128]), ALU.mult, r=["cstb", ka], w=[("Lh", d_)])
        yield
        pG = yield from galloc(FB)
        for g in range(2):
            MM(fbank[pG][:, g * 128:(g + 1) * 128], lhsT=xc[:, 4 + g, js], rhs=xc[:, 6 + g, js], r=[("xc", 4 + g), ("xc", 6 + g)], w=[FB.names[pG]])
        for d_, mk in ((0, K_LE), (1, K_GE)):
            TT("dve", GM[:, d_, :, :], fbank[pG][:, 0:256].rearrange("p (g q) -> p g q", g=2), mf(mk).unsqueeze(1).to_broadcast([128, 2, 128]),
               ALU.mult, r=[FB.names[pG], "cst"], w=["GM"])
        FB.release(pG)
        yield
        for d_, mk in ((0, K_LE), (1, K_GE)):
            p0 = yield from galloc(FB)
            p1 = yield from galloc(FB)
            for h in range(8):
                pb = p0 if h < 4 else p1
                MM(fbank[pb][:, (h % 4) * 128:(h % 4 + 1) * 128], lhsT=Lh[:, d_ * 8 + h, :], rhs=mb(mk), r=[("Lh", d_), "cstb"], w=[FB.names[pb]])
            ACT(eseg[:, d_ * 8:d_ * 8 + 4, :], fbank[p0][:, :].rearrange("p (h q) -> p h q", h=4), AF.Exp, r=[FB.names[p0]], w=[("eseg", d_, 0)])
            ACT(eseg[:, d_ * 8 + 4:d_ * 8 + 8, :], fbank[p1][:, :].rearrange("p (h q) -> p h q", h=4), AF.Exp, r=[FB.names[p1]], w=[("eseg", d_, 1)])
            FB.release(p0)
            FB.release(p1)
            for g in range(2):
                TT("dve", eseg[:, d_ * 8 + g * 4:d_ * 8 + g * 4 + 4, :], eseg[:, d_ * 8 + g * 4:d_ * 8 + g * 4 + 4, :],
                   GM[:, d_, g:g + 1, :].to_broadcast([128, 4, 128]), ALU.mult, r=[("eseg", d_, g), "GM"], w=[("eseg", d_, g)])
            yield
        tx = yield from galloc(TB)
        xs_tokmajor(tx, j)
        px = v3(tbank[tx][:, 0:512])
        TT("dve", v3(xdt[:, 0, :]), px, bc8(DT[:, 0:8]), ALU.mult, r=[TB.names[tx], kd], w=["xdt"])
        TT("dve", v3(xdt[:, 1, :]), px, bc8(DT[:, 8:16]), ALU.mult, r=[TB.names[tx], kd], w=["xdt"])
        TT("dve", mdt[:], DT[:, 0:8], ecs[:, 16:24], ALU.mult, r=[kd, "ecs"], w=["mdt"])
        TT("dve", v3(xdec[:]), px, bc8(mdt[:]), ALU.mult, r=[TB.names[tx], "mdt"], w=["xdec"])
        TB.release(tx)
        yield
        DMA(Btok[:], bt_scr[ch], r=[("bts", ch)], w=["Btok"])
        pY = yield from galloc(FB)
        for g in range(4):
            MM(fbank[pY][:, g * 128:(g + 1) * 128], lhsT=xc[:, g, js], rhs=diagD[:, g, :], start=(g == 0), stop=False,
               r=[("xc", g), "diagD"], w=[FB.names[pY]])
        for h in range(8):
            MM(fbank[pY][:, h * 64:(h + 1) * 64], lhsT=eseg[:, h, :], rhs=xdt[:, 0, h * 64:(h + 1) * 64], start=False, stop=False,
               r=[("eseg", 0, h // 4), "xdt"], w=[FB.names[pY]])
            MM(fbank[pY][:, h * 64:(h + 1) * 64], lhsT=eseg[:, 8 + h, :], rhs=xdt[:, 1, h * 64:(h + 1) * 64], start=False, stop=(h == 7),
               r=[("eseg", 1, h // 4), "xdt"], w=[FB.names[pY]])
        yield
        pOf = yield from galloc(FB)
        pOb = yield from galloc(FB)
        for g in range(2):
            MM(fbank[pOf][:, g * 256:(g + 1) * 256], lhsT=xc[:, 6 + g, js], rhs=Hfb[:, g * 256:(g + 1) * 256], r=[("xc", 6 + g), "Hfb"], w=[FB.names[pOf]])
            MM(fbank[pOb][:, g * 256:(g + 1) * 256], lhsT=xc[:, 6 + g, js], rhs=Hio[:, g * 256:(g + 1) * 256], r=[("xc", 6 + g), "Hio"], w=[FB.names[pOb]])
        TT("dve", v3(ysb[:]), v3(fbank[pOf][:, :]), bc8(ecs[:, 0:8]), ALU.mult, r=[FB.names[pOf], "ecs"], w=["ysb"])
        TT("dve", v3(yt2[:]), v3(fbank[pOb][:, :]), bc8(ecs[:, 8:16]), ALU.mult, r=[FB.names[pOb], "ecs"], w=["yt2"])
        FB.release(pOf)
        FB.release(pOb)
        TT("dve", ysb[:], ysb[:], yt2[:], ALU.add, r=["ysb", "yt2"], w=["ysb"])
        TT("dve", ysb[:], ysb[:], fbank[pY][:, :], ALU.add, r=["ysb", FB.names[pY]], w=["ysb"])
        FB.release(pY)
        yield
        pst = yield from galloc(FB)
        for g in range(2):
            MM(fbank[pst][:, g * 256:(g + 1) * 256], lhsT=Btok[:, g * 128:(g + 1) * 128], rhs=xdec[:, g * 256:(g + 1) * 256],
               r=["Btok", "xdec"], w=[FB.names[pst]])
        TT("dve", v3(Hf[:]), v3(Hf[:]), bc8(ecs[:, 24:32]), ALU.mult, r=["Hf", "ecs"], w=["Hf"])
        TT("dve", Hf[:], Hf[:], fbank[pst][:, :], ALU.add, r=["Hf", FB.names[pst]], w=["Hf"])
        FB.release(pst)
        CP("act", Hfb[:], Hf[:], r=["Hf"], w=["Hfb"])
        yield
        TT("dve", ysb[:], ysb[:], sz[p][:], ALU.mult, r=["ysb", f"sz{p}"], w=["ysb"])
        ACT(xdec[:], ysb[:], AF.Square, r=["ysb"], w=["xdec", "ssy"], accum_out=ssy[:, 0:1])
        rstd_ops(ssy[:], rsy[:], 512, r=["ssy"], w=["rsy"])
        ACT(ymix[p][:, 0:512], ysb[:], AF.Copy, r=["ysb", "rsy"], w=[("ymix", p, 0)], scale=rsy[:, 0:1])

    def gen_glaB(sq, sc, j):
        ch = sc * 2 + j
        p = ch % 2
        js = slice(j * 128, (j + 1) * 128)
        DMA(Sio[:], sb_scr[ch], r=[("sbs", ch)], w=["Sio"])
        b = yield from galloc(FB)
        gates_sp(b, j, slice(0, 256))
        yield
        pgc = yield from galloc(FB)
        MM(fbank[pgc][:, 0:128], lhsT=sp_[:, 0:128], rhs=mf(K_LE), r=["sp_", "cst"], w=[FB.names[pgc]])
        MM(fbank[pgc][:, 128:256], lhsT=sp_[:, 128:256], rhs=mf(K_GE), r=["sp_", "cst"], w=[FB.names[pgc]])
        MM(fbank[pgc][:, 256:384], lhsT=mf(K_GT), rhs=sp_[:, 0:128], r=["sp_", "cst"], w=[FB.names[pgc]])
        ACT(egq[:].rearrange("p d t -> p (d t)"), fbank[pgc][:, 0:256], AF.Exp, r=[FB.names[pgc]], w=["egq"], scale=-1.0 / 16)
        ACT(egk[:].rearrange("p d t -> p (d t)"), fbank[pgc][:, 0:256], AF.Exp, r=[FB.names[pgc]], w=["egk"], scale=1.0 / 16)
        ACT(ekd[:], fbank[pgc][:, 256:384], AF.Exp, r=[FB.names[pgc]], w=["ekd"], scale=-1.0 / 16)
        ACT(egt[:, 0:1], fbank[pgc][:, 127:128], AF.Exp, r=[FB.names[pgc]], w=["egt"], scale=-1.0 / 16)
        FB.release(pgc)
        yield
        for d_ in range(2):
            STT("dve", qin[:, d_, :], gqk[:, 0, js], 32 ** -0.5, egq[:, d_, :], ALU.mult, ALU.mult, r=["gqk", "egq"], w=["qin"])
            TT("dve", kin[:, d_, :], gqk[:, 1, js], egk[:, d_, :], ALU.mult, r=["gqk", "egk"], w=["kin"])
            TT("dve", Qblk[:, d_, :].rearrange("p (h q) -> p h q", h=4), qin[:, d_:d_ + 1, :].to_broadcast([128, 4, 128]),
               cstb[:, KB_BLK4:KB_BLK4 + 512].rearrange("p (h q) -> p h q", h=4), ALU.mult, r=["qin", "cstb"], w=["Qblk"])
        yield
        pOG = yield from galloc(FB)
        MM(fbank[pOG][:, 0:256], lhsT=qin[:, 0, :], rhs=Sfb[:], start=True, stop=False, r=["qin", "Sfb"], w=[FB.names[pOG]])
        MM(fbank[pOG][:, 0:256], lhsT=qin[:, 1, :], rhs=Sio[:], start=False, stop=False, r=["qin", "Sio"], w=[FB.names[pOG]])
        for d_, mk in ((0, K_LE), (1, K_GE)):
            pAT = yield from galloc(FB)
            MM(fbank[pAT][:, :], lhsT=kin[:, d_, :], rhs=Qblk[:, d_, :], r=["kin", "Qblk"], w=[FB.names[pAT]])
            TT("dve", attm[:, d_, :].rearrange("p (h q) -> p h q", h=4), fbank[pAT][:, :].rearrange("p (h q) -> p h q", h=4),
               mf(mk).unsqueeze(1).to_broadcast([128, 4, 128]), ALU.mult, r=[FB.names[pAT], "cst"], w=["attm"])
            FB.release(pAT)
            for h in range(4):
                MM(fbank[pOG][:, h * 64:(h + 1) * 64], lhsT=attm[:, d_, h * 128:(h + 1) * 128], rhs=vtok[p][:, h * 64:(h + 1) * 64],
                   start=False, stop=(d_ == 1 and h == 3), r=["attm", f"vtok{p}"], w=[FB.names[pOG]])
            yield
        TT("dve", kdec[:], ktok[p][:], ekd[:], ALU.mult, r=[f"ktok{p}", "ekd"], w=["kdec"])
        pS = yield from galloc(FB)
        MM(fbank[pS][:, 0:256], lhsT=kdec[:], rhs=vtok[p][:], r=["kdec", f"vtok{p}"], w=[FB.names[pS]])
        TT("dve", stmp[:], fbank[pS][:, 0:256], cst[:, KF_BLKS:KF_BLKS + 256], ALU.mult, r=[FB.names[pS], "cst"], w=["stmp"])
        FB.release(pS)
        STT("dve", Sf[:], Sf[:], egt[:, 0:1], stmp[:], ALU.mult, ALU.add, r=["Sf", "egt", "stmp"], w=["Sf"])
        CP("act", Sfb[:], Sf[:], r=["Sf"], w=["Sfb"])
        yield
        for h in range(4):
            ACT(sj[:, h * 64:(h + 1) * 64], fbank[pOG][:, h * 64:(h + 1) * 64], AF.Square, r=[FB.names[pOG]], w=["sjg", "sso"],
                accum_out=sso[:, h:h + 1])
        rstd_ops(sso[:], rso[:], 64, r=["sso"], w=["rso"])
        TT("dve", v3(og[:], 4), v3(fbank[pOG][:, 0:256], 4), rso[:].unsqueeze(2).to_broadcast([128, 4, 64]), ALU.mult,
           r=[FB.names[pOG], "rso"], w=["og"])
        FB.release(pOG)
        TT("pool", ymix[p][:, 512:768], og[:], sg[p][:, 0:256], ALU.mult, r=["og", f"sg{p}"], w=[("ymix", p, 1)])

    def gen_attB(sq, sc, j):
        ch = sc * 2 + j
        p = ch % 2
        slot = sc % 2
        sample = sq["kind"] == "s"
        nch = sq["L"] // 128
        if sample:
            DMA(ropet[j][:], rope_d[ch * 128:(ch + 1) * 128, :], w=[f"ropet{j}"])
        rw, rk_ = raw[p], f"raw{p}"
        for h in range(4):
            ACT(sj[:, 256 + h * 64:256 + (h + 1) * 64], rw[:, h * 64:(h + 1) * 64], AF.Square, r=[rk_], w=["sja", "ssq"],
                accum_out=ssq[:, h:h + 1])
        rstd_ops(ssq[:], rsq[:], 64, r=["ssq"], w=["rsq"])
        TT("dve", v3(qn[:], 4), v3(rw[:], 4), rsq[:].unsqueeze(2).to_broadcast([128, 4, 64]), ALU.mult,
           r=[rk_, "rsq"], w=["qn"])
        TT("dve", v3(qn[:], 4), v3(qn[:], 4), qkw_bc[:, 0:4, :], ALU.mult, r=["qn", "qkw_bc"], w=["qn"])
        yield
        if sample:
            rope_apply(qn[:], qr1[:], 4, ropet[j], f"ropet{j}", ["qn"], ["qr1"])
            qsrc, qk_ = qr1, "qr1"
        else:
            qsrc, qk_ = qn, "qn"
        CP("act", qb[:].rearrange("p (g j e) -> p j g e", g=2, j=2), qsrc[:].rearrange("p (j g e) -> p j g e", j=2, g=2), r=[qk_], w=["qb"])
        yield
        tb = yield from galloc(TB)
        for g in range(2):
            TR(tbank[tb][:, g * 128:(g + 1) * 128], qb[:, g * 128:(g + 1) * 128], identb, r=["qb", "cstb"], w=[TB.names[tb]])
        CP("act", qT[:].rearrange("p g t -> p (g t)"), tbank[tb][:, 0:256], r=[TB.names[tb]], w=["qT"])
        TB.release(tb)
        yield
        blocks = []
        if sample:
            for bb in range(2):
                blocks.append((KTc[:, bb * 128:(bb + 1) * 128], Vc[:, bb, :, :], None, ["KTc"], ["Vc", "Vc1"]))
            for cc, mk in ((ch - 1, K_GE), (ch, None), (ch + 1, K_LE)):
                if 0 <= cc < nch:
                    blocks.append((KT[:, cc * 128:(cc + 1) * 128], Vt[:, cc, :, :], mk, [("KT", cc)], [("Vt", cc), "Vt1"]))
        else:
            for cc in range(nch):
                blocks.append((KT[:, cc * 128:(cc + 1) * 128], Vt[:, cc, :, :], None, [("KT", cc)], [("Vt", cc), "Vt1"]))
        pAV = yield from galloc(FB)
        for jh in range(2):
            for bi_, (kap, vap, mk, kdeps, vdeps) in enumerate(blocks):
                pSc = yield from galloc(FB)
                MM(fbank[pSc][:, 0:256], lhsT=kap[jh * 64:(jh + 1) * 64, :], rhs=qT[jh * 64:(jh + 1) * 64, :, :].rearrange("p g t -> p (g t)"),
                   r=kdeps + ["qT"], w=[FB.names[pSc]])
                ACT(PT[:, bi_, :], fbank[pSc][:, 0:256], AF.Exp, r=[FB.names[pSc]], w=[("PT", bi_)], scale=0.125)
                FB.release(pSc)
                if mk is not None:
                    TT("pool", PT[:, bi_, :].rearrange("p (g q) -> p g q", g=2), PT[:, bi_, :].rearrange("p (g q) -> p g q", g=2),
                       mb(mk).unsqueeze(1).to_broadcast([128, 2, 128]), ALU.mult, r=[("PT", bi_), "cstb"], w=[("PT", bi_)])
            yield
            for g in range(2):
                hh = jh * 2 + g
                for bi_, (kap, vap, mk, kdeps, vdeps) in enumerate(blocks):
                    MM(fbank[pAV][:, hh * 65:(hh + 1) * 65], lhsT=PT[:, bi_, g * 128:(g + 1) * 128], rhs=vap[:, jh, :],
                       start=(bi_ == 0), stop=(bi_ == len(blocks) - 1), r=[("PT", bi_)] + vdeps, w=[FB.names[pAV]])
            yield
        av = fbank[pAV][:, 0:260].rearrange("p (h e) -> p h e", h=4)
        TT("dve", den[:].unsqueeze(2), av[:, :, 64:65], esink[:].unsqueeze(2), ALU.add, r=[FB.names[pAV], "esink"], w=["den"])
        S.add("dve", lambda e: e.reciprocal(out=den[:], in_=den[:]), ["den"], ["den"])
        TT("dve", v3(oat[:], 4), av[:, :, 0:64], den[:].unsqueeze(2).to_broadcast([128, 4, 64]), ALU.mult, r=[FB.names[pAV], "den"], w=["oat"])
        FB.release(pAV)
        TT("pool", ymix[p][:, 768:1024], oat[:], sg[p][:, 256:512], ALU.mult, r=["oat", f"sg{p}"], w=[("ymix", p, 2)])

    def gen_out(sq, sc, j):
        ch = sc * 2 + j
        p = ch % 2
        c = sq["cond"]
        kx = ("X", sq["id"], sc)
        DMA(xtO[p][:], sq["src"][ch * 128:(ch + 1) * 128, :], r=[kx], w=[f"xtO{p}"])
        tb = yield from galloc(TB)
        for k in range(8):
            TR(tbank[tb][:, k * 128:(k + 1) * 128], ymix[p][:, k * 128:(k + 1) * 128], identb,
               r=[("ymix", p, 0), ("ymix", p, 1), ("ymix", p, 2), "cstb"], w=[TB.names[tb]])
        CP("act", ymT[:].rearrange("p k t -> p (k t)"), tbank[tb][:, :], r=[TB.names[tb]], w=["ymT"])
        TB.release(tb)
        yield
        for nb in range(2):
            po = yield from galloc(FB)
            for k in range(8):
                MM(fbank[po][:, :], lhsT=ymT[:, k, :], rhs=wout[:, k, nb * 512:(nb + 1) * 512], start=(k == 0), stop=(k == 7),
                   r=["ymT", "wout"], w=[FB.names[po]])
            TT("dve", xo[:], fbank[po][:, :], gate_bc[:, c, nb * 512:(nb + 1) * 512], ALU.mult,
               r=[FB.names[po], "gate_bc"], w=["xo"])
            FB.release(po)
            TT("pool", xtO[p][:, nb * 512:(nb + 1) * 512], xtO[p][:, nb * 512:(nb + 1) * 512], xo[:], ALU.add,
               r=["xo", f"xtO{p}"], w=[f"xtO{p}"])
            yield
        DMA(sq["dst"][ch * 128:(ch + 1) * 128, :], xtO[p][:], r=[f"xtO{p}"], w=[kx])

    def seq_setup(sq, l):
        if sq["kind"] == "s":
            DMA(ckst[:], ck_d[l].rearrange("(b p) f -> p b f", p=128), w=["og"])
            CP("dve", ckb[:], ckst[:], r=["og"], w=["ckb"])
            tb = TB.alloc()
            for bb in range(2):
                TR(tbank[tb][:, bb * 128:(bb + 1) * 128], ckb[:, bb, :], identb, r=["ckb", "cstb"], w=[TB.names[tb]])
            CP("act", KTc[:], tbank[tb][:, 0:256], r=[TB.names[tb]], w=["KTc"])
            TB.release(tb)
            DMA(ckst[:], cv_d[l].rearrange("(b p) f -> p b f", p=128), w=["og"])
            CP("dve", Vc[:, :, :, 0:64], ckst[:].rearrange("p b (j e) -> p b j e", j=2), r=["og"], w=["Vc"])
            for d_, Ht in ((0, Hf), (1, Hb)):
                DMA(sti[:], sssd_d[l, d_].rearrange("(b p) n -> p b n", p=128), w=["ysb"])
                pb = FB.alloc()
                for bb in range(4):
                    TR(fbank[pb][:, bb * 128:(bb + 1) * 128], sti[:, bb, :], identf, r=["ysb", "cst"], w=[FB.names[pb]])
                CP("dve", Ht[:], fbank[pb][:, :], r=[FB.names[pb]], w=["Hf" if d_ == 0 else "Hb"])
                FB.release(pb)
            for d_, St in ((0, Sf), (1, Sb)):
                kname = "Sf" if d_ == 0 else "Sb"
                MSET("dve", St[:], 0.0, w=[kname])
                for h in range(4):
                    DMA(St[h * 32:(h + 1) * 32, h * 64:(h + 1) * 64], sgla_d[l, d_, h * 32:(h + 1) * 32, :], w=[kname])
        else:
            for t_, kn in ((Hf, "Hf"), (Hb, "Hb"), (Sf, "Sf"), (Sb, "Sb")):
                MSET("dve", t_[:], 0.0, w=[kn])
        CP("act", Hfb[:], Hf[:], r=["Hf"], w=["Hfb"])
        CP("act", Sfb[:], Sf[:], r=["Sf"], w=["Sfb"])

    def write_states(sq, l, d_):
        Ht, hk = (Hf, "Hf") if d_ == 0 else (Hb, "Hb")
        St, sk = (Sf, "Sf") if d_ == 0 else (Sb, "Sb")
        pb = FB.alloc()
        for bb in range(4):
            TR(fbank[pb][:, bb * 128:(bb + 1) * 128], Ht[:, bb * 128:(bb + 1) * 128], identf, r=[hk, "cst"], w=[FB.names[pb]])
        CP("dve", sti[:].rearrange("p b n -> p (b n)"), fbank[pb][:, :], r=[FB.names[pb]], w=["ysb"])
        FB.release(pb)
        DMA(nssd_d[sq["pi"], l, d_].rearrange("(b p) n -> p b n", p=128), sti[:], r=["ysb"], w=[("nssd", sq["pi"], l, d_)])
        for h in range(4):
            DMA(ngla_d[sq["pi"], l, d_, h * 32:(h + 1) * 32, :], St[h * 32:(h + 1) * 32, h * 64:(h + 1) * 64], r=[sk],
                w=[("ngla", sq["pi"], l, d_, h)])

    seqs = [dict(kind="s", id="s", L=LS, cond=0, nsc=LS // TS)]
    for pi in range(NPR):
        seqs.append(dict(kind="p", id=f"p{pi}", pi=pi, L=LP, cond=1, nsc=1))
    for l in range(NL):
        load_layer(l)
        for sq in seqs:
            if sq["kind"] == "s":
                sq["src"] = xs_d if l == 0 else ys_d
                sq["dst"] = ys_d
            else:
                pi = sq["pi"]
                sq["src"] = (xp_d if l == 0 else yp_d)[pi * LP:(pi + 1) * LP, :]
                sq["dst"] = yp_d[pi * LP:(pi + 1) * LP, :]
            nsc = sq["nsc"]
            seq_setup(sq, l)
            order = list(range(nsc - 1, -1, -1))

            def mkH(i, desc):
                if i >= len(order):
                    return None
                return gen_H(sq, order[i], order[i - 1] if i > 0 else None, i + 1 < len(order), desc)

            run1(mkH(0, True))
            if len(order) > 1:
                run1(mkH(1, True))
            for i, sc in enumerate(order):
                slot = sc % 2
                DMA(ht_scr[sc], hT[slot][:, :, :].rearrange("p k t -> p (k t)"), r=hkeys(slot), w=[("hts", sc)])
                rr([gen_fm(slot, range(8), False), gen_tm(slot, 1, 1, False, sc * 2 + 1), gen_tm(slot, 0, 0, False, sc * 2)])
                DMA(xc_scr[sc], xc[:, :, :].rearrange("p g t -> p (g t)"), r=[("xc", g_) for g_ in range(8)], w=[("xcs", sc)])
                DMA(lr_scr[sc], lrT[0:32, :], r=["lrT"], w=[("lrs", sc)])
                ACT(dum[:, 0:1], cst[:, K_ONE:K_ONE + 1], AF.Ln, r=["cst"], w=["dum"])
                rr([chain(gen_ssdA(sq, sc, 1), gen_ssdA(sq, sc, 0)), chain(gen_glaA(sq, sc, 1), gen_glaA(sq, sc, 0)),
                    chain(gen_kvA(sq, sc, 1, l), gen_kvA(sq, sc, 0, l)), mkH(i + 2, True)])
            if sq["kind"] == "p":
                write_states(sq, l, 1)
            order = list(range(nsc))

            def mkH(i, desc):
                if i >= len(order):
                    return None

                def g(sc_):
                    DMA(hT[sc_ % 2][:, :, :].rearrange("p k t -> p (k t)"), ht_scr[sc_], r=[("hts", sc_)], w=hkeys(sc_ % 2))
                    yield
                return g(order[i])
            run1(mkH(0, False))
            if len(order) > 1:
                run1(mkH(1, False))
            pending_out = None
            for i, sc in enumerate(order):
                slot = sc % 2
                DMA(xc[:, :, :].rearrange("p g t -> p (g t)"), xc_scr[sc], r=[("xcs", sc)], w=[("xc", g_) for g_ in range(8)])
                DMA(lrT[0:32, :], lr_scr[sc], r=[("lrs", sc)], w=["lrT"])
                rr([gen_fm(slot, (), True, False), gen_tm(slot, 0, 0, True, sc * 2), gen_tm(slot, 1, 1, True, sc * 2 + 1), pending_out])
                ACT(dum[:, 0:1], cst[:, K_ONE:K_ONE + 1], AF.Ln, r=["cst"], w=["dum"])
                fl = {}
                rr([chain(lane(fl, "c0", gen_ssdB(sq, sc, 0)), gen_ssdB(sq, sc, 1)),
                    chain(lane(fl, "c0", gen_glaB(sq, sc, 0)), gen_glaB(sq, sc, 1)),
                    chain(lane(fl, "c0", gen_attB(sq, sc, 0)), gen_attB(sq, sc, 1)),
                    gated(fl, "c0", 3, gen_out(sq, sc, 0)), mkH(i + 2, False)])
                pending_out = gen_out(sq, sc, 1)
            run1(pending_out)
            if sq["kind"] == "p":
                write_states(sq, l, 0)
    S.emit()
    return nc, len(S.ops)


_PROG_CACHE = {}


def _get_prog(LS, NPR, NL):
    key = (LS, NPR, NL)
    if key not in _PROG_CACHE:
        _PROG_CACHE[key] = build_program(LS, NPR, NL)[0]
    return _PROG_CACHE[key]


def kernel(x_prompt, x_sample, c, cache_k, cache_v, state_ssd, state_gla, c_ctx, w_ada, b_ada,
           norm_w, w_in, conv_w, conv_b, ssd_a_log, ssd_dt_bias, ssd_d, ssd_norm_w, gla_gk_up,
           gla_gk_b, gla_norm_w, attn_q_norm, attn_k_norm, attn_sink, w_out):
    f = lambda a: np.ascontiguousarray(np.asarray(a, dtype=np.float32))
    x_prompt, x_sample = f(x_prompt), f(x_sample)
    NCORE = 8
    B, LP, _ = x_prompt.shape
    NB, LS, _ = x_sample.shape
    NL = w_in.shape[0]
    NPR = B // NCORE
    nc = _get_prog(LS, NPR, NL)
    consts = build_consts()
    rope = build_rope(LS)
    shared = {
        "rope": rope, "consts": consts,
        "w_ada": f(w_ada), "b_ada": f(b_ada), "norm_w": f(norm_w), "w_in": f(w_in), "conv_w": f(conv_w),
        "conv_b": f(conv_b), "a_log": f(ssd_a_log).reshape(NL, 16), "dt_bias": f(ssd_dt_bias).reshape(NL, 16),
        "ssd_d": f(ssd_d), "ssd_norm_w": f(ssd_norm_w), "gk_up": f(gla_gk_up), "gk_b": f(gla_gk_b).reshape(NL, 256),
        "gla_norm_w": f(gla_norm_w), "q_norm": f(attn_q_norm), "k_norm": f(attn_k_norm), "sink": f(attn_sink),
        "w_out": f(w_out),
    }
    c, c_ctx = f(c), f(c_ctx)
    cache_k, cache_v, state_ssd, state_gla = f(cache_k), f(cache_v), f(state_ssd), f(state_gla)
    in_maps = []
    for i in range(NCORE):
        b = i % NB
        m = dict(shared)
        m["xs"] = x_sample[b]
        m["xp"] = x_prompt[i * NPR:(i + 1) * NPR].reshape(NPR * LP, D)
        m["cvec"] = np.ascontiguousarray(np.stack([c[b], c_ctx], 0))
        m["ck"] = np.ascontiguousarray(cache_k[b].reshape(NL, 256, 128))
        m["cv"] = np.ascontiguousarray(cache_v[b].reshape(NL, 256, 128))
        m["sssd"] = np.ascontiguousarray(state_ssd[b].reshape(NL, 2, 512, 128))
        m["sgla"] = np.ascontiguousarray(state_gla[b].reshape(NL, 2, 128, 64))
        in_maps.append(m)
    res = run_bass_kernel_spmd(nc, in_maps, core_ids=list(range(NCORE)))
    R = res.results
    y_sample = np.stack([R[b]["ys"] for b in range(NB)], 0)
    y_prompt = np.concatenate([R[i]["yp"].reshape(NPR, LP, D) for i in range(NCORE)], 0)
    nk = np.concatenate([R[i]["nk"] for i in range(NCORE)], 0).reshape(B, NL, LP, 2, 64)
    nv = np.concatenate([R[i]["nv"] for i in range(NCORE)], 0).reshape(B, NL, LP, 2, 64)
    nssd = np.concatenate([R[i]["nssd"] for i in range(NCORE)], 0).reshape(B, NL, 2, 8, 64, 128)
    ngla = np.concatenate([R[i]["ngla"] for i in range(NCORE)], 0).reshape(B, NL, 2, 4, 32, 64)
    return (y_prompt.astype(np.float32), y_sample.astype(np.float32), nk.astype(np.float32), nv.astype(np.float32),
            nssd.astype(np.float32), ngla.astype(np.float32))
```
